# Optimizing a Trainium2 kernel written in Bass

```python
import jax, jax.numpy as jnp
from jax import lax
import numpy as np

D_MODEL = 1024
BATCH = 8
SEQ = 8192
DEPTH = 2
DEC_BATCH = 16
DEC_SEQ = 16
PAST_LEN = 1024

CHUNK = 64
LIN_BLOCK = 16
N_EVEN = (DEPTH + 1) // 2
N_ODD = DEPTH // 2
F32 = jnp.float32
EPS = 1e-6

H_A = 8
N_A = 64
A_DIM = H_A * N_A
A_W_RANK = 64
A_A_RANK = 64
A_G_RANK = 128
A_PROJ = 3 * A_DIM + A_W_RANK + A_A_RANK + A_G_RANK
A_SPLITS = (A_DIM, 2 * A_DIM, 3 * A_DIM, 3 * A_DIM + A_W_RANK, 3 * A_DIM + A_W_RANK + A_A_RANK)
RWKV_LN_EPS = 64e-5
H_B = 4
DK_B = 128
DV_B = 128
CONV_W = 4
B_CONV_DIM = H_B * (2 * DK_B + DV_B)
B_PROJ = B_CONV_DIM + 2 * H_B + H_B * DV_B
B_SPLITS = (B_CONV_DIM, B_CONV_DIM + H_B, B_CONV_DIM + 2 * H_B)
EVEN_PROJ = A_PROJ + B_PROJ
EVEN_MIX = A_DIM + H_B * DV_B
H_C = 4
DK_C = 64
DV_C = 128
C_A_RANK = 16
GLA_NORMALIZER = 16.0
C_PROJ = H_C * (2 * DK_C + 2 * DV_C) + C_A_RANK
C_SPLITS = (H_C * DK_C, 2 * H_C * DK_C, 2 * H_C * DK_C + H_C * DV_C, 2 * H_C * DK_C + H_C * DV_C + C_A_RANK)
H_D = 4
DK_D = 128
DV_D = 128
D_PROJ = H_D * (2 * DK_D + 2 * DV_D)
D_SPLITS = (H_D * DK_D, 2 * H_D * DK_D, 2 * H_D * DK_D + H_D * DV_D)
ROPE_BASE = 10000.0
ODD_PROJ = C_PROJ + D_PROJ
ODD_MIX = H_C * DV_C + H_D * DV_D
N_MEM = 256
MEM_HEADS = 4
MEM_HD = D_MODEL // MEM_HEADS
D_FF = 2816

kernel_name = 'hybrid_streaming_encoder_step'


def _rmsnorm(x, w):
    xf = x.astype(F32)
    y = xf * lax.rsqrt(jnp.mean(xf * xf, axis=-1, keepdims=True) + EPS)
    return (y * w.astype(F32)).astype(x.dtype)


def _swiglu(h, wg, wu, wd):
    return (jax.nn.silu(h @ wg) * (h @ wu)) @ wd


def _l2norm(x):
    xf = x.astype(F32)
    return xf * lax.rsqrt(jnp.sum(xf * xf, axis=-1, keepdims=True) + EPS)


def _head_rmsnorm(x, w):
    xf = x.astype(F32)
    return xf * lax.rsqrt(jnp.mean(xf * xf, axis=-1, keepdims=True) + EPS) * w.astype(F32)


def _head_layernorm(x, eps):
    xf = x.astype(F32)
    mu = jnp.mean(xf, axis=-1, keepdims=True)
    xc = xf - mu
    return xc * lax.rsqrt(jnp.mean(xc * xc, axis=-1, keepdims=True) + eps)


def _rotary(x, pos):
    d = x.shape[-1]
    inv = ROPE_BASE ** (-jnp.arange(0, d, 2, dtype=F32) / d)
    ang = pos[:, None] * inv[None, :]
    cos, sin = jnp.cos(ang)[:, None, :], jnp.sin(ang)[:, None, :]
    x1, x2 = x[..., : d // 2].astype(F32), x[..., d // 2:].astype(F32)
    return jnp.concatenate([x1 * cos - x2 * sin, x1 * sin + x2 * cos], axis=-1)


def _pad_time(x, pad):
    return jnp.pad(x, [(0, 0), (0, pad)] + [(0, 0)] * (x.ndim - 2))


def _to_blocks(x, L):
    b, t, h, d = x.shape
    return x.reshape(b, t // L, L, h, d).transpose(1, 0, 3, 2, 4)


def _to_blocks3(x, L):
    b, t, h = x.shape
    return x.reshape(b, t // L, L, h).transpose(1, 0, 3, 2)


def _from_blocks(x):
    n, b, h, l, d = x.shape
    return x.transpose(1, 0, 3, 2, 4).reshape(b, n * l, h, d)


def _chunked_decay_linear_attn(q, k, v, log_a, s0, block):
    t = q.shape[1]
    pad = (-t) % block
    q, k, v, log_a = (_to_blocks(_pad_time(z.astype(F32), pad), block) for z in (q, k, v, log_a))
    cum = jnp.cumsum(log_a, axis=3)
    total = cum[:, :, :, -1:, :]
    q_in = q * jnp.exp(cum)
    k_in = k * jnp.exp(-cum)
    k_out = k * jnp.exp(total - cum)
    causal = jnp.tril(jnp.ones((block, block), dtype=bool))
    scores = jnp.where(causal, jnp.einsum('nbhld,nbhmd->nbhlm', q_in, k_in), 0.0)
    intra = jnp.einsum('nbhlm,nbhmv->nbhlv', scores, v)
    dec = jnp.exp(jnp.swapaxes(total, -1, -2))

    def step(s, inp):
        qi, ko, vi, di, oi = inp
        o = jnp.einsum('bhld,bhdv->bhlv', qi, s) + oi
        s = s * di + jnp.einsum('bhld,bhlv->bhdv', ko, vi)
        return s, o

    s_fin, o = lax.scan(step, s0.astype(F32), (q_in, k_out, v, dec, intra))
    return _from_blocks(o)[:, :t], s_fin


def _chunked_gated_delta(q, k, v, g, beta, s0):
    t = q.shape[1]
    L = min(CHUNK, t)
    pad = (-t) % L
    q, k, v = (_to_blocks(_pad_time(z.astype(F32), pad), L) for z in (q, k, v))
    g, beta = (_to_blocks3(_pad_time(z.astype(F32), pad), L) for z in (g, beta))
    dv = v.shape[-1]
    G = jnp.cumsum(g, axis=-1)
    diff = G[..., :, None] - G[..., None, :]
    strict = jnp.tril(jnp.ones((L, L), dtype=bool), -1)
    incl = jnp.tril(jnp.ones((L, L), dtype=bool))
    gam_strict = jnp.where(strict, jnp.exp(jnp.where(strict, diff, 0.0)), 0.0)
    gam_incl = jnp.where(incl, jnp.exp(jnp.where(incl, diff, 0.0)), 0.0)
    eG = jnp.exp(G)
    m = jnp.eye(L, dtype=F32) + beta[..., :, None] * jnp.einsum('nbhld,nbhmd->nbhlm', k, k) * gam_strict
    rhs = jnp.concatenate([beta[..., None] * v, (beta * eG)[..., None] * k], axis=-1)
    sol = lax.linalg.triangular_solve(m, rhs, left_side=True, lower=True, unit_diagonal=True)
    vb, wk = sol[..., :dv], sol[..., dv:]
    qk = jnp.einsum('nbhld,nbhmd->nbhlm', q, k) * gam_incl
    qg = q * eG[..., None]
    kdec = k * jnp.exp(G[..., -1:] - G)[..., None]
    gl = eG[..., -1][..., None, None]

    def step(s, inp):
        vb_i, wk_i, qk_i, qg_i, kd_i, gl_i = inp
        u = vb_i - jnp.einsum('bhld,bhdv->bhlv', wk_i, s)
        o = jnp.einsum('bhld,bhdv->bhlv', qg_i, s) + jnp.einsum('bhlm,bhmv->bhlv', qk_i, u)
        s = gl_i * s + jnp.einsum('bhld,bhlv->bhdv', kd_i, u)
        return s, o

    s_fin, o = lax.scan(step, s0.astype(F32), (vb, wk, qk, qg, kdec, gl))
    return _from_blocks(o)[:, :t], s_fin


def _rwkv7_recurrence(r, w, k, v, a, b, s0):
    def step(s, inp):
        rt, wt, kt, vt, at, bt = inp
        sa = jnp.einsum('bhvk,bhk->bhv', s, at)
        s = s * wt[:, :, None, :] + sa[..., None] * bt[:, :, None, :] + vt[..., None] * kt[:, :, None, :]
        return s, jnp.einsum('bhvk,bhk->bhv', s, rt)

    xs = tuple(jnp.moveaxis(z.astype(F32), 1, 0) for z in (r, w, k, v, a, b))
    s_fin, y = lax.scan(step, s0.astype(F32), xs)
    return jnp.moveaxis(y, 0, 1), s_fin


def _even_mixer(h, shift0, rwkv0, conv0, delta0, W, i):
    b, t, _ = h.shape
    p = h @ W['even_w_in'][i]
    pa, pb = p[..., :A_PROJ], p[..., A_PROJ:]
    prev = jnp.concatenate([shift0.astype(pa.dtype), pa[:, :-1]], axis=1)
    xa = pa + (prev - pa) * W['rwkv_mu'][i]
    new_shift = pa[:, -1:]
    r, k, v, xw, xk_a, xg = jnp.split(xa, A_SPLITS, axis=-1)
    wlog = -jax.nn.softplus(-(W['rwkv_w0'][i] + jnp.tanh(xw) @ W['rwkv_w2'][i]).astype(F32)) - 0.5
    decay = jnp.exp(-jnp.exp(wlog))
    a = jax.nn.sigmoid((W['rwkv_a0'][i] + xk_a @ W['rwkv_a2'][i]).astype(F32))
    gate = jax.nn.sigmoid(xg) @ W['rwkv_g2'][i]
    heads = lambda z: z.astype(F32).reshape(b, t, H_A, N_A)
    r, k, v, a, decay = heads(r), heads(k), heads(v), heads(a), heads(decay)
    kk = _l2norm(k * W['rwkv_kk'][i].reshape(H_A, N_A))
    k = k * (1.0 + (a - 1.0) * W['rwkv_ka'][i].reshape(H_A, N_A))
    y, rwkv_new = _rwkv7_recurrence(r, decay, k, v, -kk, kk * a, rwkv0)
    y = _head_layernorm(y, RWKV_LN_EPS) * W['rwkv_ln_w'][i].reshape(H_A, N_A) + W['rwkv_ln_b'][i].reshape(H_A, N_A)
    y = y + jnp.sum(r * k * W['rwkv_rk'][i], axis=-1, keepdims=True) * v
    y_a = y.reshape(b, t, A_DIM) * gate
    qkv, ba, bb, z = jnp.split(pb, B_SPLITS, axis=-1)
    cat = jnp.concatenate([conv0.astype(qkv.dtype), qkv], axis=1)
    cw = W['delta_conv_w'][i]
    conv = cat[:, 0:t] * cw[0]
    for j in range(1, CONV_W):
        conv = conv + cat[:, j:j + t] * cw[j]
    new_conv = cat[:, t:]
    conv = jax.nn.silu(conv)
    q, kb, vb = jnp.split(conv, (H_B * DK_B, 2 * H_B * DK_B), axis=-1)
    q = _l2norm(q.reshape(b, t, H_B, DK_B)) * DK_B ** -0.5
    kb = _l2norm(kb.reshape(b, t, H_B, DK_B))
    vb = vb.reshape(b, t, H_B, DV_B)
    g = -jnp.exp(W['delta_A_log'][i].astype(F32)) * jax.nn.softplus((ba + W['delta_dt_bias'][i]).astype(F32))
    beta = jax.nn.sigmoid(bb.astype(F32))
    o, delta_new = _chunked_gated_delta(q, kb, vb, g, beta, delta0)
    o = _head_rmsnorm(o, W['delta_norm_w'][i]) * jax.nn.silu(z.reshape(b, t, H_B, DV_B).astype(F32))
    y_b = o.reshape(b, t, H_B * DV_B)
    y = jnp.concatenate([y_a, y_b], axis=-1).astype(h.dtype) @ W['even_w_out'][i]
    return y, new_shift, rwkv_new, new_conv, delta_new


def _odd_mixer(h, pos0, gla0, ret0, W, i):
    b, t, _ = h.shape
    p = h @ W['odd_w_in'][i]
    pc, pd = p[..., :C_PROJ], p[..., C_PROJ:]
    cq, ck, cv, cad, cg = jnp.split(pc, C_SPLITS, axis=-1)
    log_a = jax.nn.log_sigmoid((cad @ W['gla_a2'][i] + W['gla_a_bias'][i]).astype(F32)) / GLA_NORMALIZER
    q = cq.reshape(b, t, H_C, DK_C) * DK_C ** -0.5
    o, gla_new = _chunked_decay_linear_attn(q, ck.reshape(b, t, H_C, DK_C), cv.reshape(b, t, H_C, DV_C),
                                            log_a.reshape(b, t, H_C, DK_C), gla0, LIN_BLOCK)
    y_c = (_head_rmsnorm(o, W['gla_norm_w'][i]) * jax.nn.silu(cg.reshape(b, t, H_C, DV_C).astype(F32))).reshape(b, t, H_C * DV_C)
    dq, dk, dv, dg = jnp.split(pd, D_SPLITS, axis=-1)
    pos = (jnp.arange(t) + pos0).astype(F32)
    q = _rotary(dq.reshape(b, t, H_D, DK_D), pos)
    k = _rotary(dk.reshape(b, t, H_D, DK_D), pos) * DK_D ** -0.5
    log_gamma = jnp.log(1.0 - 2.0 ** (-5.0 - jnp.arange(H_D, dtype=F32)))
    la = jnp.broadcast_to(log_gamma[:, None], (b, t, H_D, 1))
    o, ret_new = _chunked_decay_linear_attn(q, k, dv.reshape(b, t, H_D, DV_D), la, ret0, LIN_BLOCK)
    y_d = (_head_layernorm(o, EPS) * jax.nn.silu(dg.reshape(b, t, H_D, DV_D).astype(F32))).reshape(b, t, H_D * DV_D)
    y = jnp.concatenate([y_c, y_d], axis=-1).astype(h.dtype) @ W['odd_w_out'][i]
    return y, gla_new, ret_new


def _mem_kv(mem, norm_w, wk, wv):
    m = _rmsnorm(mem, norm_w)
    b, n, _ = m.shape
    return (m @ wk).reshape(b, n, MEM_HEADS, MEM_HD), (m @ wv).reshape(b, n, MEM_HEADS, MEM_HD)


def _mem_attn(h, mk, mv, wq, wo):
    b, t, _ = h.shape
    q = (h @ wq).reshape(b, t, MEM_HEADS, MEM_HD)
    s = jnp.einsum('bthd,bmhd->bhtm', q, mk.astype(q.dtype)).astype(F32) * MEM_HD ** -0.5
    pr = jax.nn.softmax(s, axis=-1).astype(h.dtype)
    o = jnp.einsum('bhtm,bmhd->bthd', pr, mv.astype(h.dtype)).reshape(b, t, D_MODEL)
    return o @ wo


def _trunk(x, pos0, mem_k, mem_v, shift, rwkv, conv, delta, gla, ret, W):
    n_shift, n_rwkv, n_conv, n_delta, n_gla, n_ret = [], [], [], [], [], []
    for l in range(DEPTH):
        x = x + 0.5 * _swiglu(_rmsnorm(x, W['norm_ffn1'][l]), W['ffn1_wg'][l], W['ffn1_wu'][l], W['ffn1_wd'][l])
        h = _rmsnorm(x, W['norm_mix'][l])
        i = l // 2
        if l % 2 == 0:
            y, s1, s2, s3, s4 = _even_mixer(h, shift[i], rwkv[i], conv[i], delta[i], W, i)
            n_shift.append(s1)
            n_rwkv.append(s2)
            n_conv.append(s3)
            n_delta.append(s4)
        else:
            y, s5, s6 = _odd_mixer(h, pos0, gla[i], ret[i], W, i)
            n_gla.append(s5)
            n_ret.append(s6)
        x = x + y
        x = x + _mem_attn(_rmsnorm(x, W['norm_mem'][l]), mem_k[l], mem_v[l], W['mem_wq'][l], W['mem_wo'][l])
        x = x + 0.5 * _swiglu(_rmsnorm(x, W['norm_ffn2'][l]), W['ffn2_wg'][l], W['ffn2_wu'][l], W['ffn2_wd'][l])
    return _rmsnorm(x, W['final_norm']), (n_shift, n_rwkv, n_conv, n_delta, n_gla, n_ret)


def setup_inputs(seed: int = 0) -> dict:
    key = jax.random.key(seed)
    keys = iter(jax.random.split(key, 96))

    def nrm(shape, scale=1.0):
        return jax.random.normal(next(keys), shape, F32) * scale

    def unif(shape, lo, hi):
        return jax.random.uniform(next(keys), shape, F32, lo, hi)

    def gain(shape):
        return 1.0 + 0.05 * nrm(shape)

    D = D_MODEL
    return {
        'x_prompt': nrm((BATCH, SEQ, D)),
        'x_sample': nrm((DEC_BATCH, DEC_SEQ, D)),
        'mem_prompt': nrm((BATCH, N_MEM, D)),
        'state_rwkv_shift': nrm((N_EVEN, DEC_BATCH, 1, A_PROJ)),
        'state_rwkv': nrm((N_EVEN, DEC_BATCH, H_A, N_A, N_A), 0.3),
        'state_delta_conv': nrm((N_EVEN, DEC_BATCH, CONV_W - 1, B_CONV_DIM)),
        'state_delta': nrm((N_EVEN, DEC_BATCH, H_B, DK_B, DV_B), DK_B ** -0.5),
        'state_gla': nrm((N_ODD, DEC_BATCH, H_C, DK_C, DV_C), 0.3),
        'state_ret': nrm((N_ODD, DEC_BATCH, H_D, DK_D, DV_D), 0.3),
        'cache_mem_k': nrm((DEPTH, DEC_BATCH, N_MEM, MEM_HEADS, MEM_HD)),
        'cache_mem_v': nrm((DEPTH, DEC_BATCH, N_MEM, MEM_HEADS, MEM_HD)),
        'norm_ffn1': gain((DEPTH, D)),
        'ffn1_wg': nrm((DEPTH, D, D_FF), D ** -0.5),
        'ffn1_wu': nrm((DEPTH, D, D_FF), D ** -0.5),
        'ffn1_wd': nrm((DEPTH, D_FF, D), D_FF ** -0.5),
        'norm_mix': gain((DEPTH, D)),
        'even_w_in': nrm((N_EVEN, D, EVEN_PROJ), D ** -0.5),
        'even_w_out': nrm((N_EVEN, EVEN_MIX, D), EVEN_MIX ** -0.5),
        'rwkv_mu': unif((N_EVEN, A_PROJ), 0.0, 1.0),
        'rwkv_w0': unif((N_EVEN, A_DIM), -4.0, 0.0),
        'rwkv_w2': nrm((N_EVEN, A_W_RANK, A_DIM), 0.5 * A_W_RANK ** -0.5),
        'rwkv_a0': nrm((N_EVEN, A_DIM), 0.5),
        'rwkv_a2': nrm((N_EVEN, A_A_RANK, A_DIM), A_A_RANK ** -0.5),
        'rwkv_g2': nrm((N_EVEN, A_G_RANK, A_DIM), A_G_RANK ** -0.5),
        'rwkv_kk': 0.85 + 0.05 * nrm((N_EVEN, A_DIM)),
        'rwkv_ka': gain((N_EVEN, A_DIM)),
        'rwkv_rk': nrm((N_EVEN, H_A, N_A), 0.1),
        'rwkv_ln_w': gain((N_EVEN, A_DIM)),
        'rwkv_ln_b': nrm((N_EVEN, A_DIM), 0.02),
        'delta_conv_w': nrm((N_EVEN, CONV_W, B_CONV_DIM), 0.5),
        'delta_A_log': jnp.log(unif((N_EVEN, H_B), 1.0, 16.0)),
        'delta_dt_bias': nrm((N_EVEN, H_B), 0.1),
        'delta_norm_w': gain((N_EVEN, DV_B)),
        'odd_w_in': nrm((N_ODD, D, ODD_PROJ), D ** -0.5),
        'odd_w_out': nrm((N_ODD, ODD_MIX, D), ODD_MIX ** -0.5),
        'gla_a2': nrm((N_ODD, C_A_RANK, H_C * DK_C), C_A_RANK ** -0.5),
        'gla_a_bias': nrm((N_ODD, H_C * DK_C), 0.1),
        'gla_norm_w': gain((N_ODD, DV_C)),
        'norm_mem': gain((DEPTH, D)),
        'mem_norm_kv': gain((DEPTH, D)),
        'mem_wq': nrm((DEPTH, D, D), D ** -0.5),
        'mem_wk': nrm((DEPTH, D, D), D ** -0.5),
        'mem_wv': nrm((DEPTH, D, D), D ** -0.5),
        'mem_wo': nrm((DEPTH, D, D), D ** -0.5),
        'norm_ffn2': gain((DEPTH, D)),
        'ffn2_wg': nrm((DEPTH, D, D_FF), D ** -0.5),
        'ffn2_wu': nrm((DEPTH, D, D_FF), D ** -0.5),
        'ffn2_wd': nrm((DEPTH, D_FF, D), D_FF ** -0.5),
        'final_norm': gain((D,)),
    }


def reference(x_prompt, x_sample, mem_prompt, state_rwkv_shift, state_rwkv, state_delta_conv, state_delta,
              state_gla, state_ret, cache_mem_k, cache_mem_v, norm_ffn1, ffn1_wg, ffn1_wu, ffn1_wd, norm_mix,
              even_w_in, even_w_out, rwkv_mu, rwkv_w0, rwkv_w2, rwkv_a0, rwkv_a2, rwkv_g2, rwkv_kk, rwkv_ka,
              rwkv_rk, rwkv_ln_w, rwkv_ln_b, delta_conv_w, delta_A_log, delta_dt_bias, delta_norm_w, odd_w_in,
              odd_w_out, gla_a2, gla_a_bias, gla_norm_w, norm_mem, mem_norm_kv, mem_wq, mem_wk, mem_wv, mem_wo,
              norm_ffn2, ffn2_wg, ffn2_wu, ffn2_wd, final_norm):
    W = dict(norm_ffn1=norm_ffn1, ffn1_wg=ffn1_wg, ffn1_wu=ffn1_wu, ffn1_wd=ffn1_wd, norm_mix=norm_mix,
             even_w_in=even_w_in, even_w_out=even_w_out, rwkv_mu=rwkv_mu, rwkv_w0=rwkv_w0, rwkv_w2=rwkv_w2,
             rwkv_a0=rwkv_a0, rwkv_a2=rwkv_a2, rwkv_g2=rwkv_g2, rwkv_kk=rwkv_kk, rwkv_ka=rwkv_ka,
             rwkv_rk=rwkv_rk, rwkv_ln_w=rwkv_ln_w, rwkv_ln_b=rwkv_ln_b, delta_conv_w=delta_conv_w,
             delta_A_log=delta_A_log, delta_dt_bias=delta_dt_bias, delta_norm_w=delta_norm_w,
             odd_w_in=odd_w_in, odd_w_out=odd_w_out, gla_a2=gla_a2, gla_a_bias=gla_a_bias,
             gla_norm_w=gla_norm_w, norm_mem=norm_mem, mem_wq=mem_wq, mem_wo=mem_wo, norm_ffn2=norm_ffn2,
             ffn2_wg=ffn2_wg, ffn2_wu=ffn2_wu, ffn2_wd=ffn2_wd, final_norm=final_norm)
    dt = x_prompt.dtype
    bp = x_prompt.shape[0]
    pk, pv = [], []
    for l in range(DEPTH):
        mk, mv = _mem_kv(mem_prompt, mem_norm_kv[l], mem_wk[l], mem_wv[l])
        pk.append(mk)
        pv.append(mv)
    z_shift = [jnp.zeros((bp, 1, A_PROJ), F32)] * N_EVEN
    z_rwkv = [jnp.zeros((bp, H_A, N_A, N_A), F32)] * N_EVEN
    z_conv = [jnp.zeros((bp, CONV_W - 1, B_CONV_DIM), F32)] * N_EVEN
    z_delta = [jnp.zeros((bp, H_B, DK_B, DV_B), F32)] * N_EVEN
    z_gla = [jnp.zeros((bp, H_C, DK_C, DV_C), F32)] * N_ODD
    z_ret = [jnp.zeros((bp, H_D, DK_D, DV_D), F32)] * N_ODD
    y_prompt, ps = _trunk(x_prompt, 0, pk, pv, z_shift, z_rwkv, z_conv, z_delta, z_gla, z_ret, W)
    y_sample, ss = _trunk(x_sample, PAST_LEN,
                          [cache_mem_k[l] for l in range(DEPTH)], [cache_mem_v[l] for l in range(DEPTH)],
                          [state_rwkv_shift[i] for i in range(N_EVEN)], [state_rwkv[i] for i in range(N_EVEN)],
                          [state_delta_conv[i] for i in range(N_EVEN)], [state_delta[i] for i in range(N_EVEN)],
                          [state_gla[i] for i in range(N_ODD)], [state_ret[i] for i in range(N_ODD)], W)
    p_shift = jnp.stack(ps[0]).astype(dt)
    p_rwkv = jnp.stack(ps[1]).astype(dt)
    p_conv = jnp.stack(ps[2]).astype(dt)
    p_delta = jnp.stack(ps[3]).astype(dt)
    p_gla = jnp.stack(ps[4]).astype(dt)
    p_ret = jnp.stack(ps[5]).astype(dt)
    p_mem_k = jnp.stack(pk).astype(dt)
    p_mem_v = jnp.stack(pv).astype(dt)
    s_shift = jnp.stack(ss[0]).astype(dt)
    s_rwkv = jnp.stack(ss[1]).astype(dt)
    s_conv = jnp.stack(ss[2]).astype(dt)
    s_delta = jnp.stack(ss[3]).astype(dt)
    s_gla = jnp.stack(ss[4]).astype(dt)
    s_ret = jnp.stack(ss[5]).astype(dt)
    return (y_prompt, y_sample, p_shift, p_rwkv, p_conv, p_delta, p_gla, p_ret, p_mem_k, p_mem_v,
            s_shift, s_rwkv, s_conv, s_delta, s_gla, s_ret)
```

```python
import numpy as np
import concourse.bass as bass
import concourse.mybir as mybir
from concourse.bass_utils import run_bass_kernel_spmd

F32 = mybir.dt.float32
BF16 = mybir.dt.bfloat16
ALU = mybir.AluOpType
AF = mybir.ActivationFunctionType
AX = mybir.AxisListType

COMPUTE = ('pe', 'act', 'dve', 'pool')
SE_DIST = 10**9


class Op:
    __slots__ = ('eng', 'fn', 'reads', 'writes', 'is_dma', 'group', 'gcount', 'inc', 'seq',
                 'deps', 'pos', 'idx')

    def __init__(self, eng, fn, reads, writes, is_dma=False, group=None):
        self.eng = eng
        self.fn = fn
        self.reads = reads
        self.writes = writes
        self.is_dma = is_dma
        self.group = group
        self.gcount = 0
        self.inc = False
        self.seq = 0
        self.deps = []
        self.pos = 0
        self.idx = 0


class Prog:
    def __init__(self, nc):
        self.nc = nc
        self.ops = []
        self.state = {}
        self.groups = {}
        self.eng_ops = {e: [] for e in ('pe', 'act', 'dve', 'pool', 'sp')}

    @staticmethod
    def _tok(t):
        if isinstance(t, tuple):
            return t[0], t[1]
        return t, None

    def _conf(self, name, key):
        d = self.state.get(name)
        if not d:
            return []
        if key is None:
            return list(d.values())
        out = []
        if None in d:
            out.append(d[None])
        if key in d:
            out.append(d[key])
        return out

    def _record(self, op):
        deps = set()
        for t in op.reads:
            name, key = self._tok(t)
            for ent in self._conf(name, key):
                if ent[0] is not None:
                    deps.add(ent[0])
                if name == 'ps':
                    for r in ent[1]:
                        if r.eng != op.eng:
                            deps.add(r)
        for t in op.writes:
            name, key = self._tok(t)
            for ent in self._conf(name, key):
                if ent[0] is not None:
                    deps.add(ent[0])
                for r in ent[1]:
                    deps.add(r)
        deps.discard(op)
        for t in op.reads:
            name, key = self._tok(t)
            d = self.state.setdefault(name, {})
            ent = d.setdefault(key, [None, []])
            ent[1].append(op)
        for t in op.writes:
            name, key = self._tok(t)
            d = self.state.setdefault(name, {})
            if key is None:
                d.clear()
            d[key] = [op, []]
        op.deps = [(d, (self.groups[d.group] if d.is_dma else 0)) for d in deps]
        op.idx = len(self.ops)
        self.ops.append(op)
        op.pos = len(self.eng_ops[op.eng])
        self.eng_ops[op.eng].append(op)

    def add(self, eng, fn, r=(), w=()):
        op = Op(eng, fn, list(r), list(w))
        self._record(op)
        return op

    def dma(self, eng, out, in_, r=(), w=(), group=None, **kw):
        assert group is not None
        op = Op(eng, None, list(r), list(w), is_dma=True, group=group)
        op.fn = (out, in_, kw)
        self._record(op)
        self.groups[group] = self.groups.get(group, 0) + 1
        op.gcount = self.groups[group]
        return op

    def emit(self, final_groups=None):
        nc = self.nc
        for op in self.ops:
            for d, _ in op.deps:
                if d.is_dma:
                    continue
                if d.eng == op.eng and not op.is_dma:
                    if d.eng == 'pe':
                        continue
                    if op.pos - d.pos > SE_DIST:
                        continue
                    raw = any(self._overlap(t, d.writes) for t in op.reads) or \
                        any(self._overlap(t, d.writes) for t in op.writes)
                    if not raw:
                        continue
                d.inc = True
        for e in COMPUTE:
            c = 0
            for op in self.eng_ops[e]:
                if op.inc:
                    c += 1
                op.seq = c
        import contextlib
        with contextlib.ExitStack() as st:
            sems = {}
            for e in COMPUTE:
                sems[e] = st.enter_context(nc.semaphore('s_' + e))
            gsem = {}
            for g in self.groups:
                gsem[g] = st.enter_context(nc.semaphore('g_' + g))
            self.n_sems = len(sems) + len(gsem)
            block = st.enter_context(nc.Block())
            ops_by = self.eng_ops
            groups_total = dict(self.groups)

            def run(eng_name, eng):
                known = {}
                for op in ops_by[eng_name]:
                    need = {}
                    for d, gsnap in op.deps:
                        if d.is_dma:
                            k = ('g', d.group)
                            v = 16 * gsnap
                            if d.group == 'const' or d.group.startswith('cvt'):
                                v = 16 * groups_total[d.group]
                        else:
                            if not d.inc:
                                continue
                            if d.eng == eng_name and d.eng == 'pe':
                                continue
                            if d.eng == eng_name and not op.is_dma and op.pos - d.pos > SE_DIST:
                                continue
                            k = ('e', d.eng)
                            v = d.seq
                        if need.get(k, 0) < v:
                            need[k] = v
                    for k, v in need.items():
                        if known.get(k, 0) >= v:
                            continue
                        known[k] = v
                        s = gsem[k[1]] if k[0] == 'g' else sems[k[1]]
                        eng.wait_ge(s, v)
                    if op.is_dma:
                        out, in_, kw = op.fn
                        eng.dma_start(out=out, in_=in_, **kw).then_inc(gsem[op.group], 16)
                    else:
                        ins = op.fn(eng)
                        if op.inc:
                            ins.then_inc(sems[eng_name], 1)
                if eng_name == 'sp':
                    for g, tot in groups_total.items():
                        eng.wait_ge(gsem[g], 16 * tot)

            @block.sync
            def _(e):
                run('sp', e)

            @block.tensor
            def _(e):
                run('pe', e)

            @block.scalar
            def _(e):
                run('act', e)

            @block.vector
            def _(e):
                run('dve', e)

            @block.gpsimd
            def _(e):
                run('pool', e)

    def _overlap(self, t, toks):
        n, k = self._tok(t)
        for t2 in toks:
            n2, k2 = self._tok(t2)
            if n == n2 and (k is None or k2 is None or k == k2):
                return True
        return False


D = 1024
KC = 8
DFF = 2816
FC = 22
NMEM = 256
EPS = 1e-6
A_PROJ = 1792
B_CONV = 1536
EVEN_PROJ = 3848
ODD_PROJ = 3600
C_PROJ = 1552
NSLOT = 3
SLOT_BYTES = 8192
WORK_BYTES = 127232 + 8192
PAST_LEN = 1024
I32 = mybir.dt.int32


class Builder:
    def __init__(self, seq, n_samp=2, samp_t=16, use=('ffn', 'attn', 'odd', 'even'), do_samp=True):
        self.seq = seq
        self.n_samp = n_samp
        self.samp_t = samp_t
        self.use = use
        self.do_samp = do_samp
        nc = bass.Bass("TRN2", target_bir_lowering=False)
        self.nc = nc
        self.P = Prog(nc)
        self._uid = 0
        self.cvt_group = 'cvt0'
        self.ps_i = 0
        self.slot_i = 0
        self.wnames = set()
        self.col0 = 0
        self.woff = 0
        self.in_names = []
        self.out_names = []
        self.build()

    def din(self, name, shape, dtype=F32):
        self.in_names.append(name)
        return self.nc.dram_tensor(name, list(shape), dtype, kind="ExternalInput").ap()

    def dout(self, name, shape, dtype=F32):
        self.out_names.append(name)
        return self.nc.dram_tensor(name, list(shape), dtype, kind="ExternalOutput").ap()

    def dscr(self, name, shape, dtype=BF16):
        return self.nc.dram_tensor(name, list(shape), dtype, kind="Internal").ap()

    def sb(self, name, shape, dtype=F32):
        return self.nc.alloc_sbuf_tensor(name, list(shape), dtype)

    def wk(self, name, shape, dtype=F32):
        esz = 4 if dtype in (F32, I32) else 2
        n = int(np.prod(shape[1:]))
        nbytes = (n * esz + 31) // 32 * 32
        off = self.woff
        self.woff += nbytes
        assert self.woff <= WORK_BYTES, (name, self.woff)
        v = self.work[:, off // 4:(off + nbytes) // 4]
        if dtype != F32:
            v = v.bitcast(dtype)
        v = v[:, 0:n]
        if len(shape) == 3:
            v = v.rearrange("p (a b) -> p a b", a=shape[1])
        elif len(shape) == 4:
            v = v.rearrange("p (a b c) -> p a b c", a=shape[1], b=shape[2])
        if shape[0] < 128:
            v = v[0:shape[0]]
        self.wnames.add(name)
        return v

    def phase(self, keep=0):
        names = sorted(self.wnames)
        self.add('pool', lambda e: e.memset(self.dummy[:, 0:1], 0.0), w=names + ['dummy'])
        self.woff = keep

    def ps(self):
        i = self.ps_i
        self.ps_i = (self.ps_i + 1) % 8
        return i, self.psb[i]

    def add(self, eng, fn, r=(), w=()):
        return self.P.add(eng, fn, r, w)

    def ev_eng(self):
        self._uid += 1
        return 'act' if self._uid % 2 else 'dve'

    def copy(self, eng, out, in_, r, w):
        if eng == 'act':
            return self.add('act', lambda e: e.activation(out=out, in_=in_, func=AF.Copy), r, w)
        return self.add(eng, lambda e: e.tensor_copy(out=out, in_=in_), r, w)

    def mm(self, out, lhsT, rhs, start, stop, r, w):
        return self.add('pe', lambda e: e.matmul(out, lhsT=lhsT, rhs=rhs, start=start, stop=stop), r, w)

    def tr(self, out, in_, ident, r, w):
        return self.add('pe', lambda e: e.transpose(out, in_, ident), r, w)

    def wtile(self, name, srcs, shape):
        scr = self.dscr('scr_' + name, shape)
        for idx, src in srcs:
            dst = scr if idx is None else scr[:, idx]
            self.P.dma('pool', dst, src, w=[('scr_' + name, idx)], group=self.cvt_group)
        return (name, scr, shape)

    def wslab(self, name, w2d, c0, c1):
        kc = w2d.shape[0] // 128
        return self.wtile(name, [(None, w2d[:, c0:c1].rearrange("(c p) j -> p c j", p=128))], [128, kc, c1 - c0])

    def wload(self, wt):
        name, scr, shape = wt
        s = self.slot_i
        self.slot_i = (self.slot_i + 1) % NSLOT
        n = int(np.prod(shape[1:]))
        assert n * 2 <= SLOT_BYTES, (name, n)
        flat = self.wslot[s][:, 0:n]
        if len(shape) == 3:
            view = flat.rearrange("p (a b) -> p a b", a=shape[1])
        elif len(shape) == 4:
            view = flat.rearrange("p (a b c) -> p a b c", a=shape[1], b=shape[2])
        else:
            view = flat
        self.P.dma('sp', view, scr, r=['scr_' + name], w=['ws%d' % s], group='ws%d' % s)
        return view, 'ws%d' % s

    def build(self):
        nc = self.nc
        seq = self.seq
        ns = self.n_samp
        st = self.samp_t
        self.io = io = {}
        io['xp'] = self.din('xp', [seq, D])
        io['memp'] = self.din('memp', [NMEM, D])
        io['xs'] = self.din('xs', [ns, st, D])
        io['ck'] = self.din('ck', [2, ns, NMEM, D])
        io['cv'] = self.din('cv', [2, ns, NMEM, D])
        io['yp'] = self.dout('yp', [seq, D])
        io['ys'] = self.dout('ys', [ns, st, D])
        io['p_mk'] = self.dout('p_mk', [2, NMEM, D])
        io['p_mv'] = self.dout('p_mv', [2, NMEM, D])
        self.w_in = {}
        wl = [('norm_ffn1', [2, D]), ('norm_ffn2', [2, D]), ('norm_mix', [2, D]), ('norm_mem', [2, D]),
              ('mem_norm_kv', [2, D]), ('final_norm', [D]),
              ('ffn1_wg', [2, D, DFF]), ('ffn1_wu', [2, D, DFF]), ('ffn1_wd', [2, DFF, D]),
              ('ffn2_wg', [2, D, DFF]), ('ffn2_wu', [2, D, DFF]), ('ffn2_wd', [2, DFF, D]),
              ('mem_wq', [2, D, D]), ('mem_wk', [2, D, D]), ('mem_wv', [2, D, D]), ('mem_wo', [2, D, D])]
        for nm, shp in wl:
            self.w_in[nm] = self.din(nm, shp)
        self.mixer_io()
        self.psb = [nc.alloc_psum_tensor('ps%d' % i, [128, 512], F32) for i in range(8)]
        self.wslot = [self.sb('wslot%d' % i, [128, SLOT_BYTES // 2], BF16) for i in range(NSLOT)]
        self.work = self.sb('work', [128, WORK_BYTES // 4], F32)
        self.dummy = self.sb('dummy_t', [128, 8], F32)
        self.ident = self.sb('ident', [128, 128], F32)
        self.identb = self.sb('identb', [128, 128], BF16)
        self.ones_bf = self.sb('ones_bf', [128, 128], BF16)
        self.epsc = self.sb('epsc', [128, 4], F32)
        for t, nm in ((self.ident, 'ident'), (self.identb, 'identb')):
            self.add('pool', lambda e, t=t: e.memset(t[:], 0.0), w=[nm])
            self.add('pool', lambda e, t=t: e.affine_select(out=t[:], in_=t[:], pattern=[[-1, 128]],
                                                            compare_op=ALU.not_equal, fill=1.0, base=0,
                                                            channel_multiplier=1), r=[nm], w=[nm])
        self.add('pool', lambda e: e.memset(self.ones_bf[:], 1.0), w=['ones_bf'])
        self.add('pool', lambda e: e.memset(self.epsc[:, 0:1], EPS), w=['epsc'])
        self.add('pool', lambda e: e.memset(self.epsc[:, 1:2], 64e-5), w=['epsc'])
        self.add('pool', lambda e: e.memset(self.epsc[:, 2:3], 1.0), w=['epsc'])
        self.add('pool', lambda e: e.memset(self.epsc[:, 3:4], float(np.pi)), w=['epsc'])
        self.ncol = {}
        nlist = [('norm_ffn1', 0), ('norm_ffn1', 1), ('norm_ffn2', 0), ('norm_ffn2', 1), ('norm_mix', 0), ('norm_mix', 1),
                 ('norm_mem', 0), ('norm_mem', 1), ('mem_norm_kv', 0), ('mem_norm_kv', 1), ('final_norm', None)]
        self.ncols = self.sb('ncols', [128, len(nlist), KC], F32)
        for i, (nm, l) in enumerate(nlist):
            src = self.w_in[nm][l] if l is not None else self.w_in[nm]
            self.P.dma('sp', self.ncols[:, i, :], src.rearrange("(c p) -> p c", p=128), w=[('ncols', i)], group='const',
                       allow_slow_non_contiguous=True)
            self.ncol[(nm, l)] = (i, self.ncols[:, i, :])
        self.x = self.sb('x', [128, KC, 512], F32)
        self.mixer_consts()
        self.ffn_w = {}
        self.att_w = {}
        for l in range(2):
            self.cvt_group = 'cvt_a%d' % l
            for nm in ('mem_wq', 'mem_wk', 'mem_wv', 'mem_wo'):
                self.att_w[(l, nm)] = [self.wslab('l%d%s%d' % (l, nm, t), self.w_in[nm][l], t * 512, (t + 1) * 512)
                                       for t in range(2)]
        for l in range(2):
            for f in (1, 2):
                self.cvt_group = 'cvt_f%d%d' % (l, f)
                wg = self.w_in['ffn%d_wg' % f][l]
                wu = self.w_in['ffn%d_wu' % f][l]
                wd = self.w_in['ffn%d_wd' % f][l]
                gu = [self.wtile('l%df%dgu%d' % (l, f, t),
                                 [(0, wg[:, t * 256:(t + 1) * 256].rearrange("(c p) j -> p c j", p=128)),
                                  (1, wu[:, t * 256:(t + 1) * 256].rearrange("(c p) j -> p c j", p=128))],
                                 [128, 2, KC, 256]) for t in range(11)]
                dn = [self.wslab('l%df%dd%d' % (l, f, t), wd, t * 128, (t + 1) * 128) for t in range(8)]
                self.ffn_w[(l, f)] = (gu, dn)
            if l == 0:
                self.mixer_weights(('even',))
        self.mixer_weights(('odd',))
        self.scr_kT = [[self.dscr('scr_kT_%d_%d' % (s, l), [128, KC, NMEM]) for l in range(2)] for s in range(1 + ns)]
        self.scr_v = [[self.dscr('scr_v_%d_%d' % (s, l), [128, 2, D]) for l in range(2)] for s in range(1 + ns)]
        self.woff = 0
        self.h = self.wk('h', [128, KC, 512], BF16)
        self.sq = self.wk('sq', [128, KC, 512], BF16)
        self.rstd = self.wk('rstd', [128, 512], F32)
        self.common_end = self.woff
        if 'attn' in self.use or 'memkv' in self.use:
            self.memkv_prompt()
        self.run_seq(0, io['xp'], io['yp'], seq, 0)
        if self.do_samp:
            self.run_samples()
        print('sbuf remaining', self.nc.sbuf_bytes_remaining)
        self.P.emit()

    def run_samples(self):
        io = self.io
        ns, st = self.n_samp, self.samp_t
        NN = ns * st
        if 'attn' in self.use:
            for s in range(ns):
                self.memkv_sample(s)
        self.phase(self.common_end)
        self.load_x(io['xs'].rearrange("s t d -> (s t) d"), NN)
        for l in range(2):
            if 'ffn' in self.use:
                self.ffn(l, 1, NN)
            part = 'even' if l == 0 else 'odd'
            if part in self.use:
                for s in range(ns):
                    self.col0 = s * st
                    self.seq_begin(1 + s, st, parts=(part,))
                    if l == 0:
                        self.even_mixer(1 + s, st, True)
                    else:
                        self.odd_mixer(1 + s, st, PAST_LEN, True)
                    self.seq_end(1 + s, parts=(part,))
                self.col0 = 0
            if 'attn' in self.use:
                for s in range(ns):
                    self.col0 = s * st
                    self.attn(1 + s, l, st)
                self.col0 = 0
            if 'ffn' in self.use:
                self.ffn(l, 2, NN)
        self.store_y(io['ys'].rearrange("s t d -> (s t) d"), NN)

    def run_seq(self, sid, xsrc, ydst, T, pos0):
        N = min(512, T)
        self.seq_begin(sid, T)
        for ti in range(T // N):
            self.phase(self.common_end)
            self.load_x(xsrc[ti * N:(ti + 1) * N, :], N)
            self.trunk(sid, N, pos0 + ti * N, ti == T // N - 1)
            self.store_y(ydst[ti * N:(ti + 1) * N, :], N)
        self.seq_end(sid)

    def to_fm(self, src_tm, srcname, dst, dstname, N, nb, L):
        for c in range(KC):
            bi, bank = self.ps()
            for b in range(nb):
                self.tr(bank[:, b * 128:b * 128 + L], src_tm[:L, b, c * 128:(c + 1) * 128], self.ident[:L, :L],
                        r=[srcname, 'ident'], w=[('ps', bi)])
            self.copy(self.ev_eng(), dst[:, c, :N], bank[:, :N], r=[('ps', bi)], w=[(dstname, c)])

    def load_x(self, src, N):
        nb = max(1, N // 128)
        L = min(N, 128)
        xin = self.wk('xin', [128, 4, D], F32)
        self.P.dma('pool', xin[:L, 0:nb, :], src.rearrange("(b p) d -> p b d", p=L), w=['xin'], group='xin')
        self.to_fm(xin, 'xin', self.x, 'x', N, nb, L)

    def store_y(self, dst, N):
        nb = max(1, N // 128)
        L = min(N, 128)
        self.phase(self.common_end)
        yn = self.wk('yn', [128, KC, 512], F32)
        yo = self.wk('yo', [128, 4, D], F32)
        self.rmsnorm(self.x, 'x', ('final_norm', None), N, out=yn, outname='yn')
        for b in range(nb):
            for half in range(2):
                bi, bank = self.ps()
                for cc in range(4):
                    c = half * 4 + cc
                    self.tr(bank[:L, cc * 128:(cc + 1) * 128], yn[:, c, b * 128:b * 128 + L], self.ident[:],
                            r=[('yn', c), 'ident'], w=[('ps', bi)])
                self.copy(self.ev_eng(), yo[:L, b, half * 512:(half + 1) * 512], bank[:L, :],
                          r=[('ps', bi)], w=['yo'])
        self.P.dma('pool', dst.rearrange("(b p) d -> p b d", p=L), yo[:L, 0:nb, :], r=['yo'], w=['yout'], group='yout')

    def rmsnorm(self, src, srcname, ncol_key, N, out=None, outname='h', KCn=KC):
        out = self.h if out is None else out
        ci, wcol = self.ncol[ncol_key]
        bi, bank = self.ps()
        c0 = self.col0 if srcname == 'x' else 0
        srcv = src
        if c0:
            src = src[:, :, c0:c0 + N]
        for c in range(KCn):
            self.add('act', lambda e, c=c: e.activation(out=self.sq[:, c, :N], in_=src[:, c, :N], func=AF.Square),
                     r=[(srcname, c)], w=[('sq', c)])
            self.mm(bank[:, :N], self.ones_bf[:], self.sq[:, c, :N], c == 0, c == KCn - 1,
                    r=[('sq', c), 'ones_bf'], w=[('ps', bi)])
        self.add('act', lambda e: e.activation(out=self.rstd[:, :N], in_=bank[:, :N], func=AF.Ln,
                                               bias=self.epsc[:, 0:1], scale=1.0 / D),
                 r=[('ps', bi), 'epsc'], w=['rstd'])
        self.add('act', lambda e: e.activation(out=self.rstd[:, :N], in_=self.rstd[:, :N], func=AF.Exp, scale=-0.5), r=['rstd'], w=['rstd'])
        for c in range(KCn):
            self.add('dve', lambda e, c=c: e.scalar_tensor_tensor(out=out[:, c, :N], in0=src[:, c, :N], scalar=wcol[:, c:c + 1],
                                                                  in1=self.rstd[:, :N], op0=ALU.mult, op1=ALU.mult),
                     r=[(srcname, c), 'rstd', ('ncols', ci)], w=[(outname, c)])

    def ffn(self, l, f, N):
        self.phase(self.common_end)
        gu = self.wk('gu', [128, FC, 512], BF16)
        sg = [self.wk('sg%d' % i, [128, 512], F32) for i in range(2)]
        gu_t, dn_t = self.ffn_w[(l, f)]
        self.rmsnorm(self.x, 'x', ('norm_ffn%d' % f, l), N)
        for t in range(11):
            wv, wtok = self.wload(gu_t[t])
            for j in range(2):
                fch = 2 * t + j
                bg, pg = self.ps()
                bu, pu = self.ps()
                for gi, (bi, bank) in enumerate(((bg, pg), (bu, pu))):
                    for kc in range(KC):
                        self.mm(bank[:, :N], wv[:, gi, kc, j * 128:(j + 1) * 128], self.h[:, kc, :N], kc == 0, kc == KC - 1,
                                r=[wtok, ('h', kc)], w=[('ps', bi)])
                sgb = sg[fch % 2]
                sgn = 'sg%d' % (fch % 2)
                self.add('act', lambda e, pg=pg, sgb=sgb: e.activation(out=sgb[:, :N], in_=pg[:, :N], func=AF.Silu),
                         r=[('ps', bg)], w=[sgn])
                self.add('dve', lambda e, pu=pu, sgb=sgb, fch=fch: e.tensor_tensor(out=gu[:, fch, :N], in0=pu[:, :N],
                                                                               in1=sgb[:, :N], op=ALU.mult),
                         r=[('ps', bu), sgn], w=[('gu', fch)])
        for t in range(8):
            wv, wtok = self.wload(dn_t[t])
            for j in range(1):
                m = t
                bi, bank = self.ps()
                for fc in range(FC):
                    self.mm(bank[:, :N], wv[:, fc, :], gu[:, fc, :N], fc == 0, fc == FC - 1,
                            r=[wtok, ('gu', fc)], w=[('ps', bi)])
                self.add('dve', lambda e, m=m, bank=bank: e.scalar_tensor_tensor(
                    out=self.x[:, m, :N], in0=bank[:, :N], scalar=0.5, in1=self.x[:, m, :N], op0=ALU.mult, op1=ALU.add),
                    r=[('ps', bi), ('x', m)], w=[('x', m)])

    def linear_resid(self, wts, src, srcname, N):
        for t in range(2):
            wv, wtok = self.wload(wts[t])
            for j in range(4):
                m = 4 * t + j
                bi, bank = self.ps()
                for kc in range(KC):
                    self.mm(bank[:, :N], wv[:, kc, j * 128:(j + 1) * 128], src[:, kc, :N], kc == 0, kc == KC - 1,
                            r=[wtok, (srcname, kc)], w=[('ps', bi)])
                c0 = self.col0
                self.add('dve', lambda e, m=m, bank=bank, c0=c0: e.tensor_tensor(out=self.x[:, m, c0:c0 + N], in0=bank[:, :N],
                                                                                 in1=self.x[:, m, c0:c0 + N], op=ALU.add),
                         r=[('ps', bi), ('x', m)], w=[('x', m)])

    def memkv_prompt(self):
        self.phase(self.common_end)
        mtm = self.wk('mtm', [128, 2, D], F32)
        mfm = self.wk('mfm', [128, KC, NMEM], F32)
        mh = self.wk('mh', [128, KC, NMEM], BF16)
        kTb = self.wk('kTb', [128, KC, NMEM], BF16)
        ktm = self.wk('ktm', [128, 2, D], F32)
        vtm = self.wk('vtm32', [128, 2, D], F32)
        vtb = self.wk('vtb', [128, 2, D], BF16)
        self.P.dma('pool', mtm, self.io['memp'].rearrange("(b p) d -> p b d", p=128), w=['mtm'], group='xin')
        import os
        LIM = float(os.environ.get('KLIM', 99))
        if LIM < 1:
            return
        self.to_fm(mtm, 'mtm', mfm, 'mfm', NMEM, 2, 128)
        if LIM < 2:
            return
        for l in range(2):
            self.rmsnorm(mfm, 'mfm', ('mem_norm_kv', l), NMEM, out=mh, outname='mh')
            if LIM < 3:
                continue
            wk = self.att_w[(l, 'mem_wk')]
            wv_ = self.att_w[(l, 'mem_wv')]
            for t in range(2):
                wv, wtok = self.wload(wk[t])
                for j in range(4):
                    dc = 4 * t + j
                    bi, bank = self.ps()
                    for kc in range(KC):
                        self.mm(bank[:, :NMEM], wv[:, kc, j * 128:(j + 1) * 128], mh[:, kc, :], kc == 0, kc == KC - 1,
                                r=[wtok, ('mh', kc)], w=[('ps', bi)])
                    self.copy(self.ev_eng(), kTb[:, dc, :], bank[:, :NMEM], r=[('ps', bi)], w=[('kTb', dc)])
                for mc in range(2 if LIM >= 3.2 else 0):
                    bi, bank = self.ps()
                    for kc in range(KC):
                        self.mm(bank[:, :], mh[:, kc, mc * 128:(mc + 1) * 128], wv[:, kc, :], kc == 0, kc == KC - 1,
                                r=[wtok, ('mh', kc)], w=[('ps', bi)])
                    self.copy(self.ev_eng(), ktm[:, mc, t * 512:(t + 1) * 512], bank[:, :], r=[('ps', bi)], w=['ktm'])
            for t in range(2 if LIM >= 3.4 else 0):
                wv, wtok = self.wload(wv_[t])
                for mc in range(2):
                    bi, bank = self.ps()
                    for kc in range(KC):
                        self.mm(bank[:, :], mh[:, kc, mc * 128:(mc + 1) * 128], wv[:, kc, :], kc == 0, kc == KC - 1,
                                r=[wtok, ('mh', kc)], w=[('ps', bi)])
                    if LIM != 3.5:
                        self.copy('dve', vtm[:, mc, t * 512:(t + 1) * 512], bank[:, :], r=[('ps', bi)], w=['vtm32'])
                    if LIM != 3.6:
                        self.copy('act', vtb[:, mc, t * 512:(t + 1) * 512], bank[:, :], r=[('ps', bi)], w=['vtb'])
            if LIM < 4:
                continue
            self.P.dma('pool', self.scr_kT[0][l], kTb, r=['kTb'], w=['scr_kT_0_%d' % l], group='mkv')
            self.P.dma('pool', self.scr_v[0][l], vtb, r=['vtb'], w=['scr_v_0_%d' % l], group='mkv')
            self.P.dma('pool', self.io['p_mk'][l].rearrange("(b p) d -> p b d", p=128), ktm, r=['ktm'], w=['p_mk'], group='mkv')
            self.P.dma('pool', self.io['p_mv'][l].rearrange("(b p) d -> p b d", p=128), vtm, r=['vtm32'], w=['p_mv'], group='mkv')

    def memkv_sample(self, s):
        self.phase(self.common_end)
        ktm = self.wk('ktm', [128, 2, D], F32)
        vtm = self.wk('vtm32', [128, 2, D], F32)
        kTb = self.wk('kTb', [128, KC, NMEM], BF16)
        vtb = self.wk('vtb', [128, 2, D], BF16)
        for l in range(2):
            self.P.dma('pool', ktm, self.io['ck'][l, s].rearrange("(b p) d -> p b d", p=128), w=['ktm'], group='xin')
            self.P.dma('pool', vtm, self.io['cv'][l, s].rearrange("(b p) d -> p b d", p=128), w=['vtm32'], group='xin')
            for dc in range(KC):
                bi, bank = self.ps()
                for mc in range(2):
                    self.tr(bank[:, mc * 128:(mc + 1) * 128], ktm[:, mc, dc * 128:(dc + 1) * 128], self.ident[:],
                            r=['ktm', 'ident'], w=[('ps', bi)])
                self.copy(self.ev_eng(), kTb[:, dc, :], bank[:, :NMEM], r=[('ps', bi)], w=[('kTb', dc)])
            self.copy('dve', vtb[:, 0, :], vtm[:, 0, :], r=['vtm32'], w=['vtb'])
            self.copy('act', vtb[:, 1, :], vtm[:, 1, :], r=['vtm32'], w=['vtb'])
            self.P.dma('pool', self.scr_kT[1 + s][l], kTb, r=['kTb'], w=['scr_kT_%d_%d' % (1 + s, l)], group='mkv')
            self.P.dma('pool', self.scr_v[1 + s][l], vtb, r=['vtb'], w=['scr_v_%d_%d' % (1 + s, l)], group='mkv')

    def attn(self, sid, l, N):
        self.phase(self.common_end)
        q = self.wk('q', [128, KC, 512], BF16)
        kT = self.wk('kT', [128, KC, NMEM], BF16)
        vt = self.wk('vt', [128, 2, D], BF16)
        pT = [self.wk('pT%d' % i, [128, 2, 512], BF16) for i in range(2)]
        rinv = [self.wk('rinv%d' % i, [128, 512], F32) for i in range(2)]
        o = self.wk('o', [128, KC, 512], BF16)
        self.P.dma('pool', kT, self.scr_kT[sid][l], r=['scr_kT_%d_%d' % (sid, l)], w=['kT'], group='akv')
        self.P.dma('pool', vt, self.scr_v[sid][l], r=['scr_v_%d_%d' % (sid, l)], w=['vt'], group='akv')
        self.rmsnorm(self.x, 'x', ('norm_mem', l), N)
        wq = self.att_w[(l, 'mem_wq')]
        for t in range(2):
            wv, wtok = self.wload(wq[t])
            for j in range(4):
                dc = 4 * t + j
                bi, bank = self.ps()
                for kc in range(KC):
                    self.mm(bank[:, :N], wv[:, kc, j * 128:(j + 1) * 128], self.h[:, kc, :N], kc == 0, kc == KC - 1,
                            r=[wtok, ('h', kc)], w=[('ps', bi)])
                self.copy(self.ev_eng(), q[:, dc, :N], bank[:, :N], r=[('ps', bi)], w=[('q', dc)])
        for hh in range(4):
            par = hh % 2
            for mc in range(2):
                bi, bank = self.ps()
                for j in range(2):
                    self.mm(bank[:, :N], kT[:, 2 * hh + j, mc * 128:(mc + 1) * 128], q[:, 2 * hh + j, :N], j == 0, j == 1,
                            r=['kT', ('q', 2 * hh + j)], w=[('ps', bi)])
                self.add('act', lambda e, bank=bank, mc=mc, par=par: e.activation(
                    out=pT[par][:, mc, :N], in_=bank[:, :N], func=AF.Exp, scale=1.0 / 16.0),
                    r=[('ps', bi)], w=[('pT%d' % par, mc)])
            bs, bsum = self.ps()
            for mc in range(2):
                self.mm(bsum[:, :N], self.ones_bf[:], pT[par][:, mc, :N], mc == 0, mc == 1,
                        r=[('pT%d' % par, mc), 'ones_bf'], w=[('ps', bs)])
            self.add('dve', lambda e, bsum=bsum, par=par: e.reciprocal(out=rinv[par][:, :N], in_=bsum[:, :N]),
                     r=[('ps', bs)], w=['rinv%d' % par])
            for j in range(2):
                bi, bank = self.ps()
                for mc in range(2):
                    self.mm(bank[:, :N], vt[:, mc, (2 * hh + j) * 128:(2 * hh + j + 1) * 128], pT[par][:, mc, :N],
                            mc == 0, mc == 1, r=['vt', ('pT%d' % par, mc)], w=[('ps', bi)])
                self.add('dve', lambda e, bank=bank, par=par, hh=hh, j=j: e.tensor_tensor(
                    out=o[:, 2 * hh + j, :N], in0=bank[:, :N], in1=rinv[par][:, :N], op=ALU.mult),
                    r=[('ps', bi), 'rinv%d' % par], w=[('o', 2 * hh + j)])
        self.linear_resid(self.att_w[(l, 'mem_wo')], o, 'o', N)

    def trunk(self, sid, N, pos0, last):
        for l in range(2):
            if 'ffn' in self.use:
                self.ffn(l, 1, N)
            if l == 0 and 'even' in self.use:
                self.even_mixer(sid, N, last)
            if l == 1 and 'odd' in self.use:
                self.odd_mixer(sid, N, pos0, last)
            if 'attn' in self.use:
                self.attn(sid, l, N)
            if 'ffn' in self.use:
                self.ffn(l, 2, N)

    def mixer_io(self):
        ns = self.n_samp
        io = self.io
        for nm, shp in [('even_w_in', [1, D, EVEN_PROJ]), ('even_w_out', [1, D, D]), ('rwkv_mu', [1, A_PROJ]),
                        ('rwkv_w0', [1, 512]), ('rwkv_w2', [1, 64, 512]), ('rwkv_a0', [1, 512]), ('rwkv_a2', [1, 64, 512]),
                        ('rwkv_g2', [1, 128, 512]), ('rwkv_kk', [1, 512]), ('rwkv_ka', [1, 512]), ('rwkv_rk', [1, 8, 64]),
                        ('rwkv_ln_w', [1, 512]), ('rwkv_ln_b', [1, 512]), ('delta_conv_w', [1, 4, B_CONV]),
                        ('delta_A_log', [1, 4]), ('delta_dt_bias', [1, 4]), ('delta_norm_w', [1, 128]),
                        ('odd_w_in', [1, D, ODD_PROJ]), ('odd_w_out', [1, D, D]), ('gla_a2', [1, 16, 256]),
                        ('gla_a_bias', [1, 256]), ('gla_norm_w', [1, 128])]:
            self.w_in[nm] = self.din(nm, shp)
        for nm, shp in [('shift', [A_PROJ]), ('rwkv', [8, 64, 64]), ('conv', [3, B_CONV]), ('delta', [4, 128, 128]),
                        ('gla', [4, 64, 128]), ('ret', [4, 128, 128])]:
            io['s0_' + nm] = self.din('s0_' + nm, [ns] + shp)
            io['p_' + nm] = self.dout('p_' + nm, shp)
            io['s_' + nm] = self.dout('s_' + nm, [ns] + shp)

    def cmask(self, t, name, base, cm, pat):
        self.add('pool', lambda e: e.memset(t, 1.0), w=[name])
        self.add('pool', lambda e: e.affine_select(out=t, in_=t, pattern=pat, compare_op=ALU.is_ge, fill=0.0,
                                                   base=base, channel_multiplier=cm), r=[name], w=[name])

    def bcast_load(self, name, src_row, n, dtype=F32):
        t = self.sb(name, [128, n], dtype)
        self.P.dma('sp', t[:], src_row.partition_broadcast(128), w=[name], group='const')
        return t

    def mixer_consts(self):
        W = self.w_in
        self.U1 = self.sb('U4', [128, 128], F32)
        self.Us1 = self.sb('Us4', [128, 128], F32)
        self.cmask(self.U1[:], 'U4', 0, -1, [[1, 128]])
        self.cmask(self.Us1[:], 'Us4', -1, -1, [[1, 128]])
        self.U4 = self.U1[:].unsqueeze(1).to_broadcast([128, 4, 128])
        self.Us4 = self.Us1[:].unsqueeze(1).to_broadcast([128, 4, 128])
        self.pmask = self.sb('pmask', [128, 2], F32)
        self.add('pool', lambda e: e.memset(self.pmask[:], 0.0), w=['pmask'])
        self.add('pool', lambda e: e.memset(self.pmask[0:64, 0:1], 1.0), w=['pmask'])
        self.add('pool', lambda e: e.memset(self.pmask[64:128, 1:2], 1.0), w=['pmask'])
        if 'odd' in self.use:
            self.odd_consts()
        if 'even' in self.use:
            self.even_consts()

    def odd_consts(self):
        W = self.w_in
        a2f = self.wk('a2f', [16, 256], F32)
        self.a2b = self.sb('a2b', [16, 256], BF16)
        self.P.dma('sp', a2f, W['gla_a2'][0], w=['a2f'], group='const')
        self.copy('dve', self.a2b[:], a2f, r=['a2f'], w=['a2b'])
        self.glab = self.bcast_load('glab', W['gla_a_bias'][0], 256)
        gw1 = self.bcast_load('gw4', W['gla_norm_w'][0], 128)
        self.gw4 = gw1[:].unsqueeze(1).to_broadcast([128, 4, 128])
        self.protT = self.sb('protT', [128, 128], F32)
        self.add('pool', lambda e: e.memset(self.protT[:], 0.0), w=['protT'])
        self.add('pool', lambda e: e.affine_select(out=self.protT[:], in_=self.protT[:], pattern=[[-1, 128]],
                                                   compare_op=ALU.not_equal, fill=-1.0, base=-64, channel_multiplier=1),
                 r=['protT'], w=['protT'])
        self.add('pool', lambda e: e.affine_select(out=self.protT[:], in_=self.protT[:], pattern=[[-1, 128]],
                                                   compare_op=ALU.not_equal, fill=1.0, base=64, channel_multiplier=1),
                 r=['protT'], w=['protT'])
        pi_ = self.sb('pidx_i', [128, 1], I32)
        pf = self.sb('pidx_f', [128, 1], F32)
        self.inv_col = self.sb('inv_col', [128, 1], F32)
        self.add('pool', lambda e: e.iota(pi_[0:64, :], pattern=[[0, 1]], base=0, channel_multiplier=1), w=['pidx_i'])
        self.add('pool', lambda e: e.iota(pi_[64:128, :], pattern=[[0, 1]], base=0, channel_multiplier=1), w=['pidx_i'])
        self.copy('dve', pf[:], pi_[:], r=['pidx_i'], w=['pidx_f'])
        self.add('act', lambda e: e.activation(out=self.inv_col[:], in_=pf[:], func=AF.Exp, scale=-float(np.log(10000.0)) / 64.0),
                 r=['pidx_f'], w=['inv_col'])
        self.add('dve', lambda e: e.tensor_scalar(out=self.inv_col[:], in0=self.inv_col[:], scalar1=float(1.0 / (2 * np.pi)),
                                                  scalar2=None, op0=ALU.mult), r=['inv_col'], w=['inv_col'])
        posi = self.wk('posi', [128, 512], I32)
        self.posrow = self.sb('posrow', [128, 512], F32)
        self.add('pool', lambda e: e.iota(posi, pattern=[[1, 512]], base=0, channel_multiplier=0), w=['posi'])
        self.copy('dve', self.posrow[:], posi, r=['posi'], w=['posrow'])
        lm_i = self.wk('lm_i', [128, 128], I32)
        lm_f = self.wk('lm_f', [128, 128], F32)
        self.add('pool', lambda e: e.iota(lm_i, pattern=[[1, 128]], base=0, channel_multiplier=-1), w=['lm_i'])
        self.copy('dve', lm_f, lm_i, r=['lm_i'], w=['lm_f'])
        self.gam4 = self.sb('gam4', [128, 4, 128], F32)
        self.ecum4 = self.sb('ecum4', [128, 4, 128], F32)
        self.einv4 = self.sb('einv4', [128, 4, 128], F32)
        self.lg = [float(np.log(1.0 - 2.0 ** (-5.0 - hd))) for hd in range(4)]
        for hd in range(4):
            lg = self.lg[hd]
            self.add('act', lambda e, hd=hd, lg=lg: e.activation(out=self.gam4[:, hd, :], in_=lm_f, func=AF.Exp, scale=lg),
                     r=['lm_f'], w=[('gam4', hd)])
            self.add('dve', lambda e, hd=hd: e.scalar_tensor_tensor(out=self.gam4[:, hd, :], in0=self.gam4[:, hd, :],
                                                                    scalar=float(128 ** -0.5), in1=self.U1[:],
                                                                    op0=ALU.mult, op1=ALU.mult),
                     r=[('gam4', hd), 'U4'], w=[('gam4', hd)])
            self.add('act', lambda e, hd=hd, lg=lg: e.activation(out=self.ecum4[:, hd, :], in_=self.posrow[:, 0:128], func=AF.Exp,
                                                                 scale=lg, bias=self.lgb[:, hd:hd + 1]),
                     r=['posrow', 'lgb'], w=[('ecum4', hd)]) if False else None
        self.lgb = self.sb('lgb', [128, 8], F32)
        for hd in range(4):
            self.add('pool', lambda e, hd=hd: e.memset(self.lgb[:, hd:hd + 1], self.lg[hd]), w=['lgb'])
            self.add('pool', lambda e, hd=hd: e.memset(self.lgb[:, 4 + hd:5 + hd], -self.lg[hd]), w=['lgb'])
        for hd in range(4):
            lg = self.lg[hd]
            self.add('act', lambda e, hd=hd, lg=lg: e.activation(out=self.ecum4[:, hd, :], in_=self.posrow[:, 0:128], func=AF.Exp,
                                                                 scale=lg, bias=self.lgb[:, hd:hd + 1]),
                     r=['posrow', 'lgb'], w=[('ecum4', hd)])
            self.add('act', lambda e, hd=hd, lg=lg: e.activation(out=self.einv4[:, hd, :], in_=self.posrow[:, 0:128], func=AF.Exp,
                                                                 scale=-lg, bias=self.lgb[:, 4 + hd:5 + hd]),
                     r=['posrow', 'lgb'], w=[('einv4', hd)])
        self.Sg = self.sb('Sg', [128, 2, 128], F32)
        self.Sgb = self.sb('Sgb', [128, 2, 128], BF16)
        self.Sr = self.sb('Sr', [128, 4, 128], F32)
        self.Srb = self.sb('Srb', [128, 4, 128], BF16)

    def mixer_weights(self, parts=('odd', 'even')):
        W = self.w_in
        if 'odd' in self.use and 'odd' in parts:
            self.cvt_group = 'cvt_odd'
            wi = W['odd_w_in'][0]
            cuts = [0, 512, 1024, 1040, 1552, 2064, 2576, 3088, 3600]
            self.odd_wi = [self.wslab('oddwi%d' % t, wi, cuts[t], cuts[t + 1]) for t in range(8)]
            self.odd_wo = [self.wslab('oddwo%d' % t, W['odd_w_out'][0], t * 512, (t + 1) * 512) for t in range(2)]
        if 'even' in self.use and 'even' in parts:
            self.cvt_group = 'cvt_even'
            self.even_weights()

    def seq_begin(self, sid, T, parts=('odd', 'even')):
        io = self.io
        if 'odd' in self.use and 'odd' in parts:
            if sid == 0:
                self.add('pool', lambda e: e.memset(self.Sg[:], 0.0), w=['Sg'])
                self.add('pool', lambda e: e.memset(self.Sr[:], 0.0), w=['Sr'])
            else:
                s = sid - 1
                self.P.dma('pool', self.Sg[:], io['s0_gla'][s].rearrange("(c hh) k v -> (hh k) c v", hh=2), w=['Sg'], group='st')
                self.P.dma('pool', self.Sr[:], io['s0_ret'][s].rearrange("h k v -> k h v"), w=['Sr'], group='st')
            self.copy('dve', self.Sgb[:], self.Sg[:], r=['Sg'], w=['Sgb'])
            self.copy('dve', self.Srb[:], self.Sr[:], r=['Sr'], w=['Srb'])
        if 'even' in self.use and 'even' in parts:
            self.even_seq_begin(sid, T)

    def seq_end(self, sid, parts=('odd', 'even')):
        io = self.io
        if 'odd' in self.use and 'odd' in parts:
            dg = io['p_gla'] if sid == 0 else io['s_gla'][sid - 1]
            dr = io['p_ret'] if sid == 0 else io['s_ret'][sid - 1]
            self.P.dma('pool', dg.rearrange("(c hh) k v -> (hh k) c v", hh=2), self.Sg[:], r=['Sg'], w=['o_gla'], group='sto')
            self.P.dma('pool', dr.rearrange("h k v -> k h v"), self.Sr[:], r=['Sr'], w=['o_ret'], group='sto')
        if 'even' in self.use and 'even' in parts:
            self.even_seq_end(sid)

    def proj_fm(self, wv, wtok, c0, ncols, N, evac):
        bi, bank = self.ps()
        for kc in range(KC):
            self.mm(bank[:ncols, :N], wv[:, kc, c0:c0 + ncols], self.h[:, kc, :N], kc == 0, kc == KC - 1,
                    r=[wtok, ('h', kc)], w=[('ps', bi)])
        evac(bank, bi)

    def proj_tm(self, wv, wtok, c0, ncols, b, L, evac):
        bi, bank = self.ps()
        for kc in range(KC):
            self.mm(bank[:L, :ncols], self.h[:, kc, b * L:(b + 1) * L], wv[:, kc, c0:c0 + ncols], kc == 0, kc == KC - 1,
                    r=[wtok, ('h', kc)], w=[('ps', bi)])
        evac(bank, bi)

    def range_reduce_sin(self, t, tname, out, outname, ti, tf, tg, N):
        A = self.add
        A('dve', lambda e: e.tensor_copy(out=ti[:, :N], in_=t[:, :N]), r=[tname], w=['rr_i'])
        A('dve', lambda e: e.tensor_copy(out=tf[:, :N], in_=ti[:, :N]), r=['rr_i'], w=['rr_f'])
        A('dve', lambda e: e.tensor_tensor(out=t[:, :N], in0=t[:, :N], in1=tf[:, :N], op=ALU.subtract), r=[tname, 'rr_f'], w=[tname])
        A('dve', lambda e: e.tensor_scalar(out=tg[:, :N], in0=t[:, :N], scalar1=0.5, scalar2=None, op0=ALU.is_ge), r=[tname], w=['rr_g'])
        A('dve', lambda e: e.tensor_tensor(out=t[:, :N], in0=t[:, :N], in1=tg[:, :N], op=ALU.subtract), r=[tname, 'rr_g'], w=[tname])
        A('dve', lambda e: e.tensor_scalar(out=tg[:, :N], in0=t[:, :N], scalar1=-0.5, scalar2=None, op0=ALU.is_lt), r=[tname], w=['rr_g'])
        A('dve', lambda e: e.tensor_tensor(out=t[:, :N], in0=t[:, :N], in1=tg[:, :N], op=ALU.add), r=[tname, 'rr_g'], w=[tname])
        A('act', lambda e: e.activation(out=out[:, :N], in_=t[:, :N], func=AF.Sin, scale=float(2 * np.pi)), r=[tname], w=[outname])

    def psb16(self, bank):
        return bank[:, :].bitcast(BF16)

    def odd_mixer(self, sid, N, pos0, last):
        A = self.add
        L = min(128, N)
        nb = N // L
        self.phase(self.common_end)
        qk = self.wk('qk', [128, 4, 512], F32)
        dqk = self.wk('dqk', [128, 8, 512], F32)
        cad = self.wk('cad', [128, 512], BF16)
        vc = self.wk('vc', [128, 4, 512], BF16)
        vd = self.wk('vd', [128, 4, 512], BF16)
        gc = self.wk('gc', [128, 4, 512], F32)
        gd = self.wk('gd', [128, 4, 512], F32)
        ymix = self.wk('ymix', [128, 8, 512], BF16)
        cos = self.wk('cos', [128, 512], F32)
        sin = self.wk('sin', [128, 512], F32)
        tA = self.wk('tA', [128, 512], F32)
        tB = self.wk('tB', [128, 512], F32)
        ti = self.wk('rr_i', [128, 512], I32)
        tf = self.wk('rr_f', [128, 512], F32)
        tg = self.wk('rr_g', [128, 512], F32)
        import os
        OL = float(os.environ.get('OLIM', 99))
        if OL < 1:
            return
        self.rmsnorm(self.x, 'x', ('norm_mix', 1), N)
        wi = self.odd_wi
        wv, wtok = self.wload(wi[0])
        for j in range(4):
            self.proj_fm(wv, wtok, j * 128, 128, N,
                         lambda bank, bi, j=j: self.copy(self.ev_eng(), qk[:, j, :N], bank[:, :N], r=[('ps', bi)], w=[('qk', j)]))
        wv, wtok = self.wload(wi[1])
        for b in range(nb):
            self.proj_tm(wv, wtok, 0, 512, b, L,
                         lambda bank, bi, b=b: self.copy(self.ev_eng(), vc[:L, b, :], bank[:L, :], r=[('ps', bi)], w=[('vc', b)]))
        wv, wtok = self.wload(wi[2])
        self.proj_fm(wv, wtok, 0, 16, N,
                     lambda bank, bi: self.copy('dve', cad[:16, :N], bank[:16, :N], r=[('ps', bi)], w=['cad']))
        wv, wtok = self.wload(wi[3])
        for b in range(nb):
            def ev(bank, bi, b=b):
                A('act', lambda e: e.activation(out=gc[:L, b, :], in_=bank[:L, :], func=AF.Silu), r=[('ps', bi)], w=[('gc', b)])
                gcv = gc[:L, b, :].rearrange("p (h d) -> p h d", h=4)
                A('dve', lambda e: e.tensor_tensor(out=gcv, in0=gcv, in1=self.gw4[:L], op=ALU.mult), r=[('gc', b), 'gw4'], w=[('gc', b)])
            self.proj_tm(wv, wtok, 0, 512, b, L, ev)
        for t in range(2):
            wv, wtok = self.wload(wi[4 + t])
            for j in range(4):
                self.proj_fm(wv, wtok, j * 128, 128, N,
                             lambda bank, bi, c=4 * t + j: self.copy(self.ev_eng(), dqk[:, c, :N], bank[:, :N],
                                                                     r=[('ps', bi)], w=[('dqk', c)]))
        wv, wtok = self.wload(wi[6])
        for b in range(nb):
            self.proj_tm(wv, wtok, 0, 512, b, L,
                         lambda bank, bi, b=b: self.copy(self.ev_eng(), vd[:L, b, :], bank[:L, :], r=[('ps', bi)], w=[('vd', b)]))
        wv, wtok = self.wload(wi[7])
        for b in range(nb):
            self.proj_tm(wv, wtok, 0, 512, b, L,
                         lambda bank, bi, b=b: A('act', lambda e: e.activation(out=gd[:L, b, :], in_=bank[:L, :], func=AF.Silu),
                                                 r=[('ps', bi)], w=[('gd', b)]))
        if OL < 2:
            return
        A('dve', lambda e: e.tensor_scalar(out=tA[:, :N], in0=self.posrow[:, :N], scalar1=float(pos0), scalar2=self.inv_col[:, 0:1],
                                           op0=ALU.add, op1=ALU.mult), r=['posrow', 'inv_col'], w=['tA'])
        A('dve', lambda e: e.tensor_scalar(out=tB[:, :N], in0=tA[:, :N], scalar1=0.25, scalar2=None, op0=ALU.add), r=['tA'], w=['tB'])
        self.range_reduce_sin(tA, 'tA', sin, 'sin', ti, tf, tg, N)
        self.range_reduce_sin(tB, 'tB', cos, 'cos', ti, tf, tg, N)
        if OL < 3:
            return
        for i in range(8):
            bi, bank = self.ps()
            A('pe', lambda e, i=i, bank=bank: e.matmul(bank[:, :N], lhsT=self.protT[:], rhs=dqk[:, i, :N], start=True, stop=True),
              r=[('dqk', i), 'protT'], w=[('ps', bi)])
            A('dve', lambda e, bank=bank: e.tensor_tensor(out=tA[:, :N], in0=bank[:, :N], in1=sin[:, :N], op=ALU.mult),
              r=[('ps', bi), 'sin'], w=['tA'])
            A('dve', lambda e, i=i: e.tensor_tensor(out=dqk[:, i, :N], in0=dqk[:, i, :N], in1=cos[:, :N], op=ALU.mult),
              r=[('dqk', i), 'cos'], w=[('dqk', i)])
            A('dve', lambda e, i=i: e.tensor_tensor(out=dqk[:, i, :N], in0=dqk[:, i, :N], in1=tA[:, :N], op=ALU.add),
              r=[('dqk', i), 'tA'], w=[('dqk', i)])
        if OL < 4:
            return
        ltm = self.wk('ltm', [128, 256], F32)
        epos = self.wk('epos', [128, 2, 128], F32)
        eneg = self.wk('eneg', [128, 2, 128], F32)
        qin = self.wk('qin', [128, 2, 128], BF16)
        kin = self.wk('kin', [128, 2, 128], BF16)
        qinm = self.wk('qinm', [128, 2, 2, 128], BF16)
        qinf = self.wk('qinf', [128, 2, 128], F32)
        kinf = self.wk('kinf', [128, 2, 128], F32)
        qinmf = self.wk('qinmf', [128, 2, 2, 128], F32)
        kout = self.wk('kout', [128, 2, 128], F32)
        kotm = self.wk('kotm', [128, 256], BF16)
        sT = self.wk('sT', [128, 4, 128], BF16)
        osq = self.wk('osq', [128, 4, 128], F32)
        ss = self.wk('ss', [128, 4], F32)
        ytm = self.wk('ytm', [128, 4, 128], F32)
        rqin = self.wk('rqin', [128, 4, 128], BF16)
        rq = self.wk('rq', [128, 4, 128], BF16)
        rk = self.wk('rk', [128, 4, 128], BF16)
        rko = self.wk('rko', [128, 4, 128], F32)
        rkotm = self.wk('rkotm', [128, 4, 128], BF16)
        rsT = self.wk('rsT', [128, 4, 128], BF16)
        yc = self.wk('yc', [128, 4, 128], F32)
        mu = self.wk('mu', [128, 4], F32)
        ytm2 = self.wk('ytm2', [128, 4, 128], F32)
        for b in range(nb):
            blk = slice(b * L, (b + 1) * L)
            b1, k1 = self.ps()
            self.mm(k1[:L, :256], cad[:16, blk], self.a2b[:16, :], True, True, r=['cad', 'a2b'], w=[('ps', b1)])
            A('dve', lambda e, k1=k1: e.tensor_tensor(out=ltm[:L, :], in0=k1[:L, :256], in1=self.glab[:L, :], op=ALU.add),
              r=[('ps', b1), 'glab'], w=['ltm'])
            A('act', lambda e: e.activation(out=ltm[:L, :], in_=ltm[:L, :], func=AF.Exp, scale=-1.0), r=['ltm'], w=['ltm'])
            A('act', lambda e: e.activation(out=ltm[:L, :], in_=ltm[:L, :], func=AF.Ln, bias=self.epsc[:L, 2:3]), r=['ltm', 'epsc'], w=['ltm'])
            if OL < 5:
                continue
            b2, k2 = self.ps()
            for c in range(2):
                A('pe', lambda e, c=c, k2=k2: e.matmul(k2[:, c * 128:c * 128 + L], lhsT=ltm[:L, c * 128:(c + 1) * 128],
                                                       rhs=self.U1[:L, :L], start=True, stop=True),
                  r=['ltm', 'U4'], w=[('ps', b2)])
            k2v = k2[:, 0:256].rearrange("p (c l) -> p c l", c=2)[:, :, :L]
            A('act', lambda e, k2v=k2v: e.activation(out=epos[:, :, :L], in_=k2v, func=AF.Exp, scale=-1.0 / 16.0), r=[('ps', b2)], w=['epos'])
            A('act', lambda e, k2v=k2v: e.activation(out=eneg[:, :, :L], in_=k2v, func=AF.Exp, scale=1.0 / 16.0), r=[('ps', b2)], w=['eneg'])
            if OL < 5.5:
                continue
            A('dve', lambda e, blk=blk: e.scalar_tensor_tensor(out=qinf[:, :, :L], in0=qk[:, 0:2, blk], scalar=0.125, in1=epos[:, :, :L],
                                                               op0=ALU.mult, op1=ALU.mult), r=['qk', 'epos'], w=['qinf'])
            A('dve', lambda e, blk=blk: e.tensor_tensor(out=kinf[:, :, :L], in0=qk[:, 2:4, blk], in1=eneg[:, :, :L], op=ALU.mult),
              r=['qk', 'eneg'], w=['kinf'])
            for par in range(2):
                A('dve', lambda e, par=par: e.tensor_scalar(out=qinmf[:, par, :, :L], in0=qinf[:, :, :L], scalar1=self.pmask[:, par:par + 1],
                                                            scalar2=None, op0=ALU.mult), r=['qinf', 'pmask'], w=['qinmf'])
            self.copy('act', qinm[:, :, :, :L], qinmf[:, :, :, :L], r=['qinmf'], w=['qinm'])
            for c in range(2):
                A('dve', lambda e, c=c: e.tensor_scalar(out=kout[:, c, :L], in0=kinf[:, c, :L], scalar1=epos[:, c, L - 1:L], scalar2=None,
                                                        op0=ALU.mult), r=['kinf', 'epos'], w=[('kout', c)])
            if OL < 6:
                continue
            b3, k3 = self.ps()
            for c in range(2):
                self.tr(k3[:L, c * 128:(c + 1) * 128], kout[:, c, :L], self.ident[:], r=[('kout', c), 'ident'], w=[('ps', b3)])
            self.copy('act', kotm[:L, :], k3[:L, 0:256], r=[('ps', b3)], w=['kotm'])
            if OL < 6.5:
                continue
            b4, k4 = self.ps()
            for hd in range(4):
                c, hp = hd // 2, (hd % 2) * 64
                self.mm(k4[:L, hd * 128:hd * 128 + L], kinf[:, c, :L], qinmf[:, hd % 2, c, :L], True, True,
                        r=['kinf', 'qinmf'], w=[('ps', b4)])
            k4v = k4[:, :].rearrange("p (h l) -> p h l", h=4)
            A('dve', lambda e, k4v=k4v: e.tensor_tensor(out=sT[:L, :, :L], in0=k4v[:L, :, :L], in1=self.U4[:L, :, :L], op=ALU.mult),
              r=[('ps', b4), 'U4'], w=['sT'])
            if OL < 7:
                continue
            b5, k5 = self.ps()
            for hd in range(4):
                c, hp = hd // 2, (hd % 2) * 64
                self.mm(k5[:L, hd * 128:(hd + 1) * 128], qinm[:, hd % 2, c, :L], self.Sgb[:, c, :], True, False,
                        r=['qinm', 'Sgb'], w=[('ps', b5)])
                self.mm(k5[:L, hd * 128:(hd + 1) * 128], sT[:L, hd, :L], vc[:L, b, hd * 128:(hd + 1) * 128], False, True,
                        r=['sT', ('vc', b)], w=[('ps', b5)])
            if OL < 7.5:
                continue
            b6, k6 = self.ps()
            for c in range(2):
                self.mm(k6[:, c * 256:(c + 1) * 256], kotm[:L, c * 128:(c + 1) * 128], vc[:L, b, c * 256:(c + 1) * 256], True, True,
                        r=['kotm', ('vc', b)], w=[('ps', b6)])
            for c in range(2):
                for hh in range(2):
                    hp = hh * 64
                    A('dve', lambda e, c=c, hh=hh, hp=hp, k6=k6: e.scalar_tensor_tensor(
                        out=self.Sg[hp:hp + 64, c, :], in0=self.Sg[hp:hp + 64, c, :], scalar=epos[hp:hp + 64, c, L - 1:L],
                        in1=k6[hp:hp + 64, c * 256 + hh * 128:c * 256 + (hh + 1) * 128], op0=ALU.mult, op1=ALU.add),
                        r=['Sg', 'epos', ('ps', b6), 'Sgb'], w=['Sg'])
            self.copy('act', self.Sgb[:], self.Sg[:], r=['Sg'], w=['Sgb'])
            if OL < 8:
                continue
            k5v = k5[:, :].rearrange("p (h d) -> p h d", h=4)
            A('act', lambda e, k5v=k5v: e.activation(out=osq[:L], in_=k5v[:L], func=AF.Square), r=[('ps', b5)], w=['osq'])
            A('dve', lambda e: e.tensor_reduce(out=ss[:L, :], in_=osq[:L], axis=AX.X, op=ALU.add), r=['osq'], w=['ss'])
            A('act', lambda e: e.activation(out=ss[:L, :], in_=ss[:L, :], func=AF.Ln, bias=self.epsc[:L, 0:1], scale=1.0 / 128.0),
              r=['ss', 'epsc'], w=['ss'])
            A('act', lambda e: e.activation(out=ss[:L, :], in_=ss[:L, :], func=AF.Exp, scale=-0.5), r=['ss'], w=['ss'])
            A('dve', lambda e, k5v=k5v: e.tensor_tensor(out=osq[:L], in0=k5v[:L], in1=ss[:L, :].unsqueeze(2).to_broadcast([L, 4, 128]),
                                                        op=ALU.mult), r=[('ps', b5), 'ss'], w=['osq'])
            A('dve', lambda e, b=b: e.tensor_tensor(out=ytm[:L], in0=osq[:L], in1=gc[:L, b, :].rearrange("p (h d) -> p h d", h=4),
                                                    op=ALU.mult), r=['osq', ('gc', b)], w=['ytm'])
            b7, k7 = self.ps()
            for hd in range(4):
                self.tr(k7[:, hd * 128:hd * 128 + L], ytm[:L, hd, :], self.ident[:L, :L], r=['ytm', 'ident'], w=[('ps', b7)])
            k7v = k7[:, 0:512].rearrange("p (h l) -> p h l", h=4)
            self.copy('act', ymix[:, 0:4, blk], k7v[:, :, :L], r=[('ps', b7)], w=['ymix'])
            if OL < 9:
                continue
            A('dve', lambda e, blk=blk: e.tensor_tensor(out=rqin[:, :, :L], in0=dqk[:, 0:4, blk], in1=self.ecum4[:, :, :L], op=ALU.mult),
              r=['dqk', 'ecum4'], w=['rqin'])
            self.copy('act', rq[:, :, :L], dqk[:, 0:4, blk], r=['dqk'], w=['rq'])
            self.copy('act', rk[:, :, :L], dqk[:, 4:8, blk], r=['dqk'], w=['rk'])
            for hd in range(4):
                ch = float(128 ** -0.5 * np.exp(L * self.lg[hd]))
                A('dve', lambda e, hd=hd, ch=ch, blk=blk: e.scalar_tensor_tensor(
                    out=rko[:, hd, :L], in0=dqk[:, 4 + hd, blk], scalar=ch, in1=self.einv4[:, hd, :L], op0=ALU.mult, op1=ALU.mult),
                    r=['dqk', 'einv4'], w=[('rko', hd)])
            b3, k3 = self.ps()
            for hd in range(4):
                self.tr(k3[:L, hd * 128:(hd + 1) * 128], rko[:, hd, :L], self.ident[:], r=[('rko', hd), 'ident'], w=[('ps', b3)])
            self.copy('act', rkotm[:L].rearrange("p a b -> p (a b)"), k3[:L, 0:512], r=[('ps', b3)], w=['rkotm'])
            b4, k4 = self.ps()
            for hd in range(4):
                self.mm(k4[:L, hd * 128:hd * 128 + L], dqk[:, 4 + hd, blk], dqk[:, hd, blk], True, True, r=['dqk'], w=[('ps', b4)])
            k4v = k4[:, :].rearrange("p (h l) -> p h l", h=4)
            A('dve', lambda e, k4v=k4v: e.tensor_tensor(out=rsT[:L, :, :L], in0=k4v[:L, :, :L], in1=self.gam4[:L, :, :L], op=ALU.mult),
              r=[('ps', b4), 'gam4'], w=['rsT'])
            b5, k5 = self.ps()
            for hd in range(4):
                self.mm(k5[:L, hd * 128:(hd + 1) * 128], rqin[:, hd, :L], self.Srb[:, hd, :], True, False, r=['rqin', 'Srb'], w=[('ps', b5)])
                self.mm(k5[:L, hd * 128:(hd + 1) * 128], rsT[:L, hd, :L], vd[:L, b, hd * 128:(hd + 1) * 128], False, True,
                        r=['rsT', ('vd', b)], w=[('ps', b5)])
            b6, k6 = self.ps()
            for hd in range(4):
                self.mm(k6[:, hd * 128:(hd + 1) * 128], rkotm[:L, hd, :], vd[:L, b, hd * 128:(hd + 1) * 128], True, True,
                        r=['rkotm', ('vd', b)], w=[('ps', b6)])
            for hd in range(4):
                dec = float(np.exp(L * self.lg[hd]))
                A('dve', lambda e, hd=hd, dec=dec, k6=k6: e.scalar_tensor_tensor(
                    out=self.Sr[:, hd, :], in0=self.Sr[:, hd, :], scalar=dec, in1=k6[:, hd * 128:(hd + 1) * 128],
                    op0=ALU.mult, op1=ALU.add), r=['Sr', ('ps', b6), 'Srb'], w=['Sr'])
            self.copy('act', self.Srb[:], self.Sr[:], r=['Sr'], w=['Srb'])
            k5v = k5[:, :].rearrange("p (h d) -> p h d", h=4)
            A('dve', lambda e, k5v=k5v: e.tensor_reduce(out=mu[:L, :], in_=k5v[:L], axis=AX.X, op=ALU.add), r=[('ps', b5)], w=['mu'])
            A('dve', lambda e: e.tensor_scalar(out=mu[:L, :], in0=mu[:L, :], scalar1=-1.0 / 128.0, scalar2=None, op0=ALU.mult), r=['mu'], w=['mu'])
            A('dve', lambda e, k5v=k5v: e.tensor_tensor(out=yc[:L], in0=k5v[:L], in1=mu[:L, :].unsqueeze(2).to_broadcast([L, 4, 128]),
                                                        op=ALU.add), r=[('ps', b5), 'mu'], w=['yc'])
            A('act', lambda e: e.activation(out=osq[:L], in_=yc[:L], func=AF.Square), r=['yc'], w=['osq'])
            A('dve', lambda e: e.tensor_reduce(out=ss[:L, :], in_=osq[:L], axis=AX.X, op=ALU.add), r=['osq'], w=['ss'])
            A('act', lambda e: e.activation(out=ss[:L, :], in_=ss[:L, :], func=AF.Ln, bias=self.epsc[:L, 0:1], scale=1.0 / 128.0),
              r=['ss', 'epsc'], w=['ss'])
            A('act', lambda e: e.activation(out=ss[:L, :], in_=ss[:L, :], func=AF.Exp, scale=-0.5), r=['ss'], w=['ss'])
            A('dve', lambda e: e.tensor_tensor(out=yc[:L], in0=yc[:L], in1=ss[:L, :].unsqueeze(2).to_broadcast([L, 4, 128]), op=ALU.mult),
              r=['yc', 'ss'], w=['yc'])
            A('dve', lambda e, b=b: e.tensor_tensor(out=ytm2[:L], in0=yc[:L], in1=gd[:L, b, :].rearrange("p (h d) -> p h d", h=4),
                                                    op=ALU.mult), r=['yc', ('gd', b)], w=['ytm2'])
            b7, k7 = self.ps()
            for hd in range(4):
                self.tr(k7[:, hd * 128:hd * 128 + L], ytm2[:L, hd, :], self.ident[:L, :L], r=['ytm2', 'ident'], w=[('ps', b7)])
            k7v = k7[:, 0:512].rearrange("p (h l) -> p h l", h=4)
            self.copy('act', ymix[:, 4:8, blk], k7v[:, :, :L], r=[('ps', b7)], w=['ymix'])
        self.linear_resid(self.odd_wo, ymix, 'ymix', N)

    def col_load(self, name, src_flat, nch):
        t = self.sb(name, [128, nch], F32)
        self.P.dma('sp', t[:], src_flat.rearrange("(c p) -> p c", p=128), w=[name], group='const', allow_slow_non_contiguous=True)
        return t

    def even_consts(self):
        W = self.w_in
        A = self.add
        self.C0 = float(np.exp(-0.5))
        self.mu_col = self.col_load('mu_col', W['rwkv_mu'][0], 14)
        self.a0_col = self.col_load('a0_col', W['rwkv_a0'][0], 4)
        A('dve', lambda e: e.tensor_scalar(out=self.a0_col[:], in0=self.a0_col[:], scalar1=-1.0, scalar2=None, op0=ALU.mult), r=['a0_col'], w=['a0_col'])
        self.kkw_col = self.col_load('kkw_col', W['rwkv_kk'][0], 4)
        self.ka_col = self.col_load('ka_col', W['rwkv_ka'][0], 4)
        self.rk_col = self.col_load('rk_col', W['rwkv_rk'][0].rearrange("h d -> (h d)"), 4)
        self.omka = self.sb('omka', [128, 4], F32)
        A('dve', lambda e: e.tensor_scalar(out=self.omka[:], in0=self.ka_col[:], scalar1=-1.0, scalar2=1.0, op0=ALU.mult, op1=ALU.add),
          r=['ka_col'], w=['omka'])
        self.w0_bc = self.bcast_load('w0_bc', W['rwkv_w0'][0], 512)
        self.lnw_bc = self.bcast_load('lnw_bc', W['rwkv_ln_w'][0], 512)
        self.lnb_bc = self.bcast_load('lnb_bc', W['rwkv_ln_b'][0], 512)
        tmpf = self.wk('evc_tmp', [128, 3, 512], F32)
        A('pool', lambda e: e.memset(tmpf[:], 0.0), w=['evc_tmp'])
        self.P.dma('sp', tmpf[0:64, 0, :], W['rwkv_w2'][0], w=[('evc_tmp', 0)], group='const')
        self.P.dma('sp', tmpf[64:128, 1, :], W['rwkv_a2'][0], w=[('evc_tmp', 1)], group='const')
        self.P.dma('sp', tmpf[:, 2, :], W['rwkv_g2'][0], w=[('evc_tmp', 2)], group='const')
        self.lrw = self.sb('lrw', [128, 3, 512], BF16)
        self.copy('dve', self.lrw[:], tmpf[:], r=['evc_tmp'], w=['lrw'])
        self.ones2 = self.sb('ones2', [128, 128], BF16)
        A('pool', lambda e: e.memset(self.ones2[:], 0.0), w=['ones2'])
        A('pool', lambda e: e.memset(self.ones2[0:64, 0:64], 1.0), w=['ones2'])
        A('pool', lambda e: e.memset(self.ones2[64:128, 64:128], 1.0), w=['ones2'])
        self.sel2 = self.sb('sel2', [128, 2], BF16)
        self.copy('dve', self.sel2[:], self.pmask[:], r=['pmask'], w=['sel2'])
        self.Ls1 = self.sb('Ls4', [128, 128], F32)
        self.cmask(self.Ls1[:], 'Ls4', -1, 1, [[-1, 128]])
        self.Ls4 = self.Ls1[:].unsqueeze(1).to_broadcast([128, 4, 128])
        self.ident4 = self.ident[:].unsqueeze(1).to_broadcast([128, 4, 128])
        self.ones_f = self.sb('ones_f', [128, 128], F32)
        self.add('pool', lambda e: e.memset(self.ones_f[:], 1.0), w=['ones_f'])
        self.cw_col = self.sb('cw_col', [128, 12, 4], F32)
        for j in range(4):
            self.P.dma('sp', self.cw_col[:, :, j], W['delta_conv_w'][0, j].rearrange("(c p) -> p c", p=128), w=[('cw_col', j)],
                       group='const', allow_slow_non_contiguous=True)
        self.dtb_bc = self.bcast_load('dtb_bc', W['delta_dt_bias'][0], 4)
        self.nexpA = self.bcast_load('nexpA', W['delta_A_log'][0], 4)
        A('act', lambda e: e.activation(out=self.nexpA[:], in_=self.nexpA[:], func=AF.Exp), r=['nexpA'], w=['nexpA'])
        A('dve', lambda e: e.tensor_scalar(out=self.nexpA[:], in0=self.nexpA[:], scalar1=-1.0, scalar2=None, op0=ALU.mult),
          r=['nexpA'], w=['nexpA'])
        dn1 = self.bcast_load('dnw4', W['delta_norm_w'][0], 128)
        self.dnw4 = dn1[:].unsqueeze(1).to_broadcast([128, 4, 128])
        self.Sw = self.sb('Sw', [128, 4, 64], F32)
        self.Swb = self.sb('Swb', [128, 4, 64], BF16)
        self.Sd = self.sb('Sd', [128, 4, 128], F32)
        self.Sdb = self.sb('Sdb', [128, 4, 128], BF16)
        self.shiftc = self.sb('shiftc', [128, 14], F32)
        self.convc = self.sb('convc', [128, 12, 3], F32)

    def even_weights(self):
        W = self.w_in
        wi = W['even_w_in'][0]
        cuts = [0, 512, 1024, 1536, 1792, 2304, 2816, 3328, 3336, 3848]
        self.even_wi = [self.wslab('evwi%d' % t, wi, cuts[t], cuts[t + 1]) for t in range(9)]
        self.even_wo = [self.wslab('evwo%d' % t, W['even_w_out'][0], t * 512, (t + 1) * 512) for t in range(2)]

    def even_seq_begin(self, sid, T):
        io = self.io
        A = self.add
        if sid == 0:
            for t, nm in ((self.Sw, 'Sw'), (self.Sd, 'Sd'), (self.shiftc, 'shiftc'), (self.convc, 'convc')):
                A('pool', lambda e, t=t: e.memset(t[:], 0.0), w=[nm])
        else:
            s = sid - 1
            self.phase(self.common_end)
            t1 = self.wk('sb_t1', [64, 8, 64], F32)
            t2 = self.wk('sb_t2', [14, 128], F32)
            t3 = self.wk('sb_t3', [36, 128], F32)
            self.P.dma('pool', t1, io['s0_rwkv'][s].rearrange("h v k -> v h k"), w=['sb_t1'], group='st_e')
            self.P.dma('pool', t2, io['s0_shift'][s].rearrange("(c p) -> c p", p=128), w=['sb_t2'], group='st_e')
            self.P.dma('pool', t3, io['s0_conv'][s].rearrange("j (c p) -> (j c) p", p=128), w=['sb_t3'], group='st_e')
            self.P.dma('pool', self.Sd[:], io['s0_delta'][s].rearrange("h k v -> k h v"), w=['Sd'], group='st_e')
            bi, bank = self.ps()
            for c in range(4):
                self.tr(bank[:, c * 64:(c + 1) * 64], t1[:, 2 * c:2 * c + 2, :].rearrange("p a b -> p (a b)"), self.ident[:64, :64],
                        r=['sb_t1', 'ident'], w=[('ps', bi)])
            self.copy('dve', self.Sw[:].rearrange("p a b -> p (a b)"), bank[:, 0:256], r=[('ps', bi)], w=['Sw'])
            bi, bank = self.ps()
            self.tr(bank[:, 0:14], t2[:, :], self.ident[:14, :14], r=['sb_t2', 'ident'], w=[('ps', bi)])
            self.tr(bank[:, 64:100], t3[:, :], self.ident[:36, :36], r=['sb_t3', 'ident'], w=[('ps', bi)])
            self.copy('dve', self.shiftc[:], bank[:, 0:14], r=[('ps', bi)], w=['shiftc'])
            self.copy('dve', self.convc[:].rearrange("p c j -> p j c"), bank[:, 64:100].rearrange("p (j c) -> p j c", j=3),
                      r=[('ps', bi)], w=['convc'])
        self.copy('dve', self.Swb[:], self.Sw[:], r=['Sw'], w=['Swb'])
        self.copy('dve', self.Sdb[:], self.Sd[:], r=['Sd'], w=['Sdb'])

    def even_seq_end(self, sid):
        io = self.io
        pre = 'p_' if sid == 0 else 's_'
        dst = (lambda nm: io['p_' + nm]) if sid == 0 else (lambda nm: io['s_' + nm][sid - 1])
        self.phase(self.common_end)
        t1 = self.wk('sb_t1', [64, 8, 64], F32)
        t2 = self.wk('sb_t2', [14, 128], F32)
        t3 = self.wk('sb_t3', [36, 128], F32)
        cv = self.wk('sb_cv', [128, 3, 12], F32)
        bi, bank = self.ps()
        for c in range(4):
            self.tr(bank[:64, c * 128:(c + 1) * 128], self.Sw[:, c, :], self.ident[:], r=['Sw', 'ident'], w=[('ps', bi)])
        self.copy('dve', t1[:].rearrange("p a b -> p (a b)"), bank[:64, :], r=[('ps', bi)], w=['sb_t1'])
        self.copy('dve', cv[:], self.convc[:].rearrange("p c j -> p j c"), r=['convc'], w=['sb_cv'])
        bi, bank = self.ps()
        self.tr(bank[:14, 0:128], self.shiftc[:], self.ident[:], r=['shiftc', 'ident'], w=[('ps', bi)])
        self.tr(bank[:36, 128:256], cv[:].rearrange("p a b -> p (a b)"), self.ident[:], r=['sb_cv', 'ident'], w=[('ps', bi)])
        self.copy('dve', t2[:, :], bank[:14, 0:128], r=[('ps', bi)], w=['sb_t2'])
        self.copy('dve', t3[:, :], bank[:36, 128:256], r=[('ps', bi)], w=['sb_t3'])
        self.P.dma('pool', dst('rwkv').rearrange("h v k -> v h k"), t1, r=['sb_t1'], w=['o_rwkv'], group='sto')
        self.P.dma('pool', dst('shift').rearrange("(c p) -> c p", p=128), t2, r=['sb_t2'], w=['o_shift'], group='sto')
        self.P.dma('pool', dst('conv').rearrange("j (c p) -> (j c) p", p=128), t3, r=['sb_t3'], w=['o_conv'], group='sto')
        self.P.dma('pool', dst('delta').rearrange("h k v -> k h v"), self.Sd[:], r=['Sd'], w=['o_delta'], group='sto')

    def invert(self, groups, L):
        A = self.add
        st = []
        for g in groups:
            (Pb, Pbn), (Ptb, Ptbn), (TtA, TtAn), (TtB, TtBn) = g['bufs']
            M, Mn, Mt, Mtn = g['M'], g['Mn'], g['Mt'], g['Mtn']
            A('dve', lambda e, TtA=TtA, M=M: e.tensor_tensor(out=TtA[:L, :, :L], in0=M[:L, :, :L], in1=self.ident4[:L, :, :L], op=ALU.add),
              r=[Mn, 'ident'], w=[TtAn])
            st.append(dict(P=M, Pn=Mn, Pt=Mt, Ptn=Mtn, Tt=TtA, Ttn=TtAn, To=TtB, Ton=TtBn,
                           spare=[(Pb, Pbn, Ptb, Ptbn), (M, Mn, Mt, Mtn)]))
        p = 2
        k = 0
        while p <= L // 2:
            last = (p == L // 2)
            for q in st:
                q['n'] = q['spare'][k % 2]
                P, Pn, Pt, Ptn = q['P'], q['Pn'], q['Pt'], q['Ptn']
                if not last:
                    b1, k1 = self.ps()
                    for hd in range(4):
                        self.mm(k1[:L, hd * 128:hd * 128 + L], Pt[:L, hd, :L], P[:L, hd, :L], True, True, r=[Pn, Ptn], w=[('ps', b1)])
                    q['k1'] = (b1, k1)
                b2, k2 = self.ps()
                for hd in range(4):
                    self.mm(k2[:L, hd * 128:hd * 128 + L], P[:L, hd, :L], Pt[:L, hd, :L], True, True, r=[Pn, Ptn], w=[('ps', b2)])
                q['k2'] = (b2, k2)
            for q in st:
                Pn2, Pn2n, Ptn2, Ptn2n = q['n']
                b2, k2 = q['k2']
                k2v = k2[:, :].rearrange("p (h l) -> p h l", h=4)
                self.copy('act', Ptn2[:L, :, :L], k2v[:L, :, :L], r=[('ps', b2)], w=[Ptn2n])
                if not last:
                    b1, k1 = q['k1']
                    k1v = k1[:, :].rearrange("p (h l) -> p h l", h=4)
                    self.copy('dve', Pn2[:L, :, :L], k1v[:L, :, :L], r=[('ps', b1)], w=[Pn2n])
            for q in st:
                Pn2, Pn2n, Ptn2, Ptn2n = q['n']
                b3, k3 = self.ps()
                for hd in range(4):
                    self.mm(k3[:L, hd * 128:hd * 128 + L], Ptn2[:L, hd, :L], q['Tt'][:L, hd, :L], True, True, r=[Ptn2n, q['Ttn']], w=[('ps', b3)])
                q['k3'] = (b3, k3)
            for q in st:
                b3, k3 = q['k3']
                k3v = k3[:, :].rearrange("p (h l) -> p h l", h=4)
                A('dve', lambda e, k3v=k3v, Tt=q['Tt'], To=q['To']: e.tensor_tensor(out=To[:L, :, :L], in0=k3v[:L, :, :L], in1=Tt[:L, :, :L], op=ALU.add),
                  r=[('ps', b3), q['Ttn']], w=[q['Ton']])
                Pn2, Pn2n, Ptn2, Ptn2n = q['n']
                q['P'], q['Pn'], q['Pt'], q['Ptn'] = Pn2, Pn2n, Ptn2, Ptn2n
                q['Tt'], q['Ttn'], q['To'], q['Ton'] = q['To'], q['Ton'], q['Tt'], q['Ttn']
            p *= 2
            k += 1
        for g, q in zip(groups, st):
            self.copy('act', g['Ttb'][:L, :, :L], q['Tt'][:L, :, :L], r=[q['Ttn']], w=[g['Ttn']])

    def head_rms_post(self, k5, b5, L, gz, gzn, ytm, ytmn, osq, ss, dst, dstn, blk, eps_col, nh, hd_dim):
        A = self.add
        k5v = k5[:, :].rearrange("p (h d) -> p h d", h=nh)
        A('act', lambda e: e.activation(out=osq[:L], in_=k5v[:L], func=AF.Square), r=[('ps', b5)], w=['osq'])
        A('dve', lambda e: e.tensor_reduce(out=ss[:L, :], in_=osq[:L], axis=AX.X, op=ALU.add), r=['osq'], w=['ss'])
        A('act', lambda e: e.activation(out=ss[:L, :], in_=ss[:L, :], func=AF.Ln, bias=self.epsc[:L, eps_col:eps_col + 1],
                                        scale=1.0 / hd_dim), r=['ss', 'epsc'], w=['ss'])
        A('act', lambda e: e.activation(out=ss[:L, :], in_=ss[:L, :], func=AF.Exp, scale=-0.5), r=['ss'], w=['ss'])
        A('dve', lambda e: e.tensor_tensor(out=osq[:L], in0=k5v[:L], in1=ss[:L, :].unsqueeze(2).to_broadcast([L, nh, hd_dim]),
                                           op=ALU.mult), r=[('ps', b5), 'ss'], w=['osq'])
        A('dve', lambda e: e.tensor_tensor(out=ytm[:L], in0=osq[:L], in1=gz, op=ALU.mult), r=['osq', gzn], w=[ytmn])
        b7, k7 = self.ps()
        for cc in range(4):
            self.tr(k7[:, cc * 128:cc * 128 + L], ytm[:L].rearrange("p a b -> p (a b)")[:, cc * 128:(cc + 1) * 128], self.ident[:L, :L],
                    r=[ytmn, 'ident'], w=[('ps', b7)])
        k7v = k7[:, 0:512].rearrange("p (h l) -> p h l", h=4)
        self.copy('act', dst, k7v[:, :, :L], r=[('ps', b7)], w=[dstn])

    def even_mixer(self, sid, N, last):
        A = self.add
        L = min(128, N)
        nb = N // L
        C0 = self.C0
        self.phase(self.common_end)
        ymix = self.wk('ymix', [128, 8, 512], BF16)
        keep = self.woff
        self.rmsnorm(self.x, 'x', ('norm_mix', 0), N)
        wi = self.even_wi
        pa = self.wk('pa', [128, 14, 513], F32)
        thx = self.wk('thx', [128, 512], BF16)
        sgx = self.wk('sgx', [128, 512], BF16)
        vtm = self.wk('vtm', [128, 4, 512], BF16)
        dtmp = [self.wk('dtmp%d' % i, [128, 512], F32) for i in range(2)]
        self.copy('dve', pa[:, :, 0], self.shiftc[:], r=['shiftc'], w=['pa'])
        for t in range(4):
            wv, wtok = self.wload(wi[t])
            for j in range(4 if t < 3 else 2):
                c = 4 * t + j
                self.proj_fm(wv, wtok, j * 128, 128, N,
                             lambda bank, bi, c=c: self.copy(self.ev_eng(), pa[:, c, 1:N + 1], bank[:, :N], r=[('ps', bi)], w=[('pa', c)]))
        for c in range(14):
            d = dtmp[c % 2]
            dn = 'dtmp%d' % (c % 2)
            A('pool' if c % 2 else 'dve', lambda e, c=c, d=d: e.tensor_tensor(out=d[:, :N], in0=pa[:, c, 0:N], in1=pa[:, c, 1:N + 1],
                                                                             op=ALU.subtract), r=[('pa', c), 'pa'], w=[dn])
            self.copy('act', self.shiftc[:, c:c + 1], pa[:, c, N:N + 1], r=[('pa', c)], w=['shiftc'])
            A('dve', lambda e, c=c, d=d: e.scalar_tensor_tensor(out=pa[:, c, 1:N + 1], in0=d[:, :N], scalar=self.mu_col[:, c:c + 1],
                                                                in1=pa[:, c, 1:N + 1], op0=ALU.mult, op1=ALU.add),
              r=[dn, ('pa', c), 'mu_col'], w=[('pa', c)])
        A('act', lambda e: e.activation(out=thx[0:64, :N], in_=pa[0:64, 12, 1:N + 1], func=AF.Tanh), r=[('pa', 12)], w=['thx'])
        A('act', lambda e: e.activation(out=thx[64:128, :N], in_=pa[64:128, 12, 1:N + 1], func=AF.Copy), r=[('pa', 12)], w=['thx'])
        A('act', lambda e: e.activation(out=sgx[:, :N], in_=pa[:, 13, 1:N + 1], func=AF.Sigmoid), r=[('pa', 13)], w=['sgx'])
        for b in range(nb):
            bi, bank = self.ps()
            for cc in range(4):
                self.tr(bank[:L, cc * 128:(cc + 1) * 128], pa[:, 8 + cc, 1 + b * L:1 + (b + 1) * L], self.ident[:],
                        r=[('pa', 8 + cc), 'ident'], w=[('ps', bi)])
            self.copy(self.ev_eng(), vtm[:L, b, :], bank[:L, :], r=[('ps', bi)], w=[('vtm', b)])
        sig = dtmp[0]
        epos = self.wk('epos', [128, 4, 128], F32)
        eposx = self.wk('eposx', [128, 4, 128], F32)
        eneg = self.wk('eneg', [128, 4, 128], F32)
        ag = self.wk('ag', [128, 4, 128], F32)
        kw = self.wk('kw', [128, 4, 128], F32)
        kk = self.wk('kk', [128, 4, 128], F32)
        sqk = self.wk('sqk', [128, 4, 128], BF16)
        kp = self.wk('kp', [128, 4, 128], F32)
        tf1 = self.wk('tf1', [128, 4, 128], F32)
        tf2 = self.wk('tf2', [128, 4, 128], F32)
        art = self.wk('art', [128, 2, 8, 128], BF16)
        ktb = self.wk('ktb', [128, 4, 128], BF16)
        btb = self.wk('btb', [128, 4, 128], BF16)
        kttm = self.wk('kttm', [128, 512], BF16)
        bttm = self.wk('bttm', [128, 512], BF16)
        rkk = self.wk('rkk', [128, 4, 128], BF16)
        rtmf = self.wk('rtmf', [128, 2, 4, 128], F32)
        Mh = [self.wk('M%d' % g, [128, 4, 128], F32) for g in range(2)]
        Mth = [self.wk('Mt%d' % g, [128, 4, 128], F32) for g in range(2)]
        ArbT = self.wk('ArbT', [128, 8, 128], BF16)
        AakT = self.wk('AakT', [128, 8, 128], BF16)
        ArkT = self.wk('ArkT', [128, 8, 128], BF16)
        Pb = [self.wk('Pb%d' % g, [128, 4, 128], F32) for g in range(2)]
        Ptb = [self.wk('Ptb%d' % g, [128, 4, 128], F32) for g in range(2)]
        TtA = [self.wk('TtA%d' % g, [128, 4, 128], F32) for g in range(2)]
        TtB = [self.wk('TtB%d' % g, [128, 4, 128], F32) for g in range(2)]
        Ttb = self.wk('Ttb', [128, 8, 128], BF16)
        Xb = self.wk('Xb', [128, 8, 64], BF16)
        Ub = self.wk('Ub', [128, 8, 64], BF16)
        yA = tf1.rearrange("p c (t d) -> p (c t) d", t=2)
        yB = tf2.rearrange("p c (t d) -> p (c t) d", t=2)
        st8 = self.wk('st8', [128, 8], F32)
        srk = self.wk('srk', [128, 8], F32)
        print('rwkv work end', self.woff)
        import os
        EL = float(os.environ.get('ELIM', 99))
        for b in range(nb if EL >= 1 else 0):
            blk = slice(1 + b * L, 1 + (b + 1) * L)
            bsl = slice(b * L, (b + 1) * L)
            b1, k1 = self.ps()
            self.mm(k1[:L, :], thx[:, bsl], self.lrw[:, 0, :], True, True, r=['thx', 'lrw'], w=[('ps', b1)])
            b4, k4 = self.ps()
            for c in range(4):
                self.mm(k4[:, c * 128:c * 128 + L], self.lrw[:, 1, c * 128:(c + 1) * 128], thx[:, bsl], True, True,
                        r=['lrw', 'thx'], w=[('ps', b4)])
            for c in range(4):
                A('dve', lambda e, c=c, blk=blk: e.tensor_scalar(out=kw[:, c, :L], in0=pa[:, 4 + c, blk], scalar1=self.kkw_col[:, c:c + 1],
                                                                 scalar2=None, op0=ALU.mult), r=[('pa', 4 + c), 'kkw_col'], w=[('kw', c)])
            A('act', lambda e: e.activation(out=sqk[:, :, :L], in_=kw[:, :, :L], func=AF.Square), r=['kw'], w=['sqk'])
            b5, k5 = self.ps()
            for c in range(4):
                self.mm(k5[:, c * 128:c * 128 + L], self.ones2[:], sqk[:, c, :L], True, True, r=['sqk', 'ones2'], w=[('ps', b5)])
            A('dve', lambda e, k1=k1: e.tensor_tensor(out=sig[:L, :], in0=k1[:L, :], in1=self.w0_bc[:L, :], op=ALU.add),
              r=[('ps', b1), 'w0_bc'], w=['dtmp0'])
            for c in range(4):
                A('act', lambda e, c=c, k4=k4: e.activation(out=ag[:, c, :L], in_=k4[:, c * 128:c * 128 + L], func=AF.Exp,
                                                            bias=self.a0_col[:, c:c + 1], scale=-1.0), r=[('ps', b4), 'a0_col'], w=[('ag', c)])
            A('act', lambda e: e.activation(out=sig[:L, :], in_=sig[:L, :], func=AF.Exp, scale=-1.0), r=['dtmp0'], w=['dtmp0'])
            A('dve', lambda e: e.tensor_scalar(out=ag[:, :, :L], in0=ag[:, :, :L], scalar1=1.0, scalar2=None, op0=ALU.add), r=['ag'], w=['ag'])
            A('dve', lambda e: e.reciprocal(out=ag[:, :, :L], in_=ag[:, :, :L]), r=['ag'], w=['ag'])
            A('dve', lambda e: e.tensor_scalar(out=sig[:L, :], in0=sig[:L, :], scalar1=1.0, scalar2=None, op0=ALU.add), r=['dtmp0'], w=['dtmp0'])
            A('dve', lambda e: e.reciprocal(out=sig[:L, :], in_=sig[:L, :]), r=['dtmp0'], w=['dtmp0'])
            k5v = k5[:, :].rearrange("p (c l) -> p c l", c=4)
            A('act', lambda e, k5v=k5v: e.activation(out=kk[:, :, :L], in_=k5v[:, :, :L], func=AF.Ln, bias=self.epsc[:, 0:1]),
              r=[('ps', b5), 'epsc'], w=['kk'])
            b2, k2 = self.ps()
            b3, k3 = self.ps()
            for c in range(4):
                A('pe', lambda e, c=c, k2=k2: e.matmul(k2[:, c * 128:c * 128 + L], lhsT=sig[:L, c * 128:(c + 1) * 128],
                                                       rhs=self.U1[:L, :L], start=True, stop=True), r=['dtmp0', 'U4'], w=[('ps', b2)])
                A('pe', lambda e, c=c, k3=k3: e.matmul(k3[:, c * 128:c * 128 + L], lhsT=sig[:L, c * 128:(c + 1) * 128],
                                                       rhs=self.Us1[:L, :L], start=True, stop=True), r=['dtmp0', 'Us4'], w=[('ps', b3)])
            A('act', lambda e: e.activation(out=kk[:, :, :L], in_=kk[:, :, :L], func=AF.Exp, scale=-0.5), r=['kk'], w=['kk'])
            A('dve', lambda e: e.tensor_tensor(out=kk[:, :, :L], in0=kk[:, :, :L], in1=kw[:, :, :L], op=ALU.mult), r=['kk', 'kw'], w=['kk'])
            for c in range(4):
                A('dve', lambda e, c=c: e.tensor_scalar(out=kp[:, c, :L], in0=ag[:, c, :L], scalar1=self.ka_col[:, c:c + 1],
                                                        scalar2=self.omka[:, c:c + 1], op0=ALU.mult, op1=ALU.add),
                  r=[('ag', c), 'ka_col', 'omka'], w=[('kp', c)])
            A('dve', lambda e, blk=blk: e.tensor_tensor(out=kp[:, :, :L], in0=kp[:, :, :L], in1=pa[:, 4:8, blk], op=ALU.mult),
              r=['kp', 'pa'], w=['kp'])
            for c in range(4):
                A('dve', lambda e, c=c, blk=blk: e.scalar_tensor_tensor(out=rkk[:, c, :L], in0=pa[:, c, blk], scalar=self.rk_col[:, c:c + 1],
                                                                        in1=kp[:, c, :L], op0=ALU.mult, op1=ALU.mult),
                  r=[('pa', c), 'kp', 'rk_col'], w=[('rkk', c)])
            b6, k6 = self.ps()
            for c in range(4):
                self.mm(k6[:L, 2 * c:2 * c + 2], rkk[:, c, :L], self.sel2[:], True, True, r=[('rkk', c), 'sel2'], w=[('ps', b6)])
            k2v = k2[:, :].rearrange("p (c l) -> p c l", c=4)
            k3v = k3[:, :].rearrange("p (c l) -> p c l", c=4)
            A('act', lambda e, k3v=k3v: e.activation(out=eposx[:, :, :L], in_=k3v[:, :, :L], func=AF.Exp, scale=-C0), r=[('ps', b3)], w=['eposx'])
            A('act', lambda e, k2v=k2v: e.activation(out=epos[:, :, :L], in_=k2v[:, :, :L], func=AF.Exp, scale=-C0), r=[('ps', b2)], w=['epos'])
            A('act', lambda e, k2v=k2v: e.activation(out=eneg[:, :, :L], in_=k2v[:, :, :L], func=AF.Exp, scale=C0), r=[('ps', b2)], w=['eneg'])
            self.copy('dve', srk[:L, :], k6[:L, 0:8], r=[('ps', b6)], w=['srk'])
            A('dve', lambda e: e.scalar_tensor_tensor(out=tf1[:, :, :L], in0=kk[:, :, :L], scalar=-1.0, in1=eposx[:, :, :L],
                                                      op0=ALU.mult, op1=ALU.mult), r=['kk', 'eposx'], w=['tf1'])
            A('dve', lambda e, blk=blk: e.tensor_tensor(out=tf2[:, :, :L], in0=pa[:, 0:4, blk], in1=epos[:, :, :L], op=ALU.mult),
              r=['pa', 'epos'], w=['tf2'])
            artv = art[:].rearrange("p q (c t) l -> p q c t l", t=2)
            for par in range(2):
                A('dve', lambda e, par=par: e.tensor_scalar(out=artv[:, par, :, 0, :L], in0=tf1[:, :, :L], scalar1=self.pmask[:, par:par + 1],
                                                            scalar2=None, op0=ALU.mult), r=['tf1', 'pmask'], w=['art'])
                A('dve', lambda e, par=par: e.tensor_scalar(out=rtmf[:, par, :, :L], in0=tf2[:, :, :L], scalar1=self.pmask[:, par:par + 1],
                                                            scalar2=None, op0=ALU.mult), r=['tf2', 'pmask'], w=['rtmf'])
                self.copy('act', artv[:, par, :, 1, :L], rtmf[:, par, :, :L], r=['rtmf'], w=['art'])
            A('dve', lambda e: e.tensor_tensor(out=tf1[:, :, :L], in0=kp[:, :, :L], in1=eneg[:, :, :L], op=ALU.mult), r=['kp', 'eneg', 'art'], w=['tf1'])
            A('dve', lambda e: e.tensor_tensor(out=tf2[:, :, :L], in0=kk[:, :, :L], in1=ag[:, :, :L], op=ALU.mult), r=['kk', 'ag', 'art'], w=['tf2'])
            A('dve', lambda e: e.tensor_tensor(out=tf2[:, :, :L], in0=tf2[:, :, :L], in1=eneg[:, :, :L], op=ALU.mult), r=['tf2', 'eneg'], w=['tf2'])
            self.copy('act', ktb[:, :, :L], tf1[:, :, :L], r=['tf1'], w=['ktb'])
            self.copy('act', btb[:, :, :L], tf2[:, :, :L], r=['tf2'], w=['btb'])
            for (src, srcn, dstt, dstn) in ((tf1, 'tf1', kttm, 'kttm'), (tf2, 'tf2', bttm, 'bttm')):
                b7, k7 = self.ps()
                for c in range(4):
                    self.tr(k7[:L, c * 128:(c + 1) * 128], src[:, c, :L], self.ident[:], r=[srcn, 'ident'], w=[('ps', b7)])
                self.copy(self.ev_eng(), dstt[:L, :], k7[:L, :], r=[('ps', b7)], w=[dstn])
            if EL < 2:
                continue
            for c in range(4):
                g = c // 2
                b8, k8 = self.ps()
                b9, k9 = self.ps()
                for hh in range(2):
                    if L == 128:
                        rhs = art[:, hh, 2 * c:2 * c + 2, :].rearrange("p t l -> p (t l)")
                        self.mm(k8[:L, hh * 256:(hh + 1) * 256], btb[:, c, :L], rhs, True, True, r=['btb', 'art'], w=[('ps', b8)])
                        self.mm(k9[:L, hh * 256:(hh + 1) * 256], ktb[:, c, :L], rhs, True, True, r=['ktb', 'art'], w=[('ps', b9)])
                    else:
                        for t in range(2):
                            rhs = art[:, hh, 2 * c + t, :L]
                            self.mm(k8[:L, hh * 256 + t * 128:hh * 256 + t * 128 + L], btb[:, c, :L], rhs, True, True,
                                    r=['btb', 'art'], w=[('ps', b8)])
                            self.mm(k9[:L, hh * 256 + t * 128:hh * 256 + t * 128 + L], ktb[:, c, :L], rhs, True, True,
                                    r=['ktb', 'art'], w=[('ps', b9)])
                k8v = k8[:, :].rearrange("p (h t l) -> p h t l", h=2, t=2)
                k9v = k9[:, :].rearrange("p (h t l) -> p h t l", h=2, t=2)
                hs = slice(2 * (c % 2), 2 * (c % 2) + 2)
                h8 = slice(2 * c, 2 * c + 2)
                A('dve', lambda e, k8v=k8v, g=g, hs=hs: e.tensor_tensor(out=Mh[g][:L, hs, :L], in0=k8v[:L, :, 0, :L], in1=self.Us4[:L, 0:2, :L],
                                                                       op=ALU.mult), r=[('ps', b8), 'Us4'], w=['M%d' % g])
                A('dve', lambda e, k8v=k8v, h8=h8: e.tensor_tensor(out=ArbT[:L, h8, :L], in0=k8v[:L, :, 1, :L], in1=self.U4[:L, 0:2, :L],
                                                                  op=ALU.mult), r=[('ps', b8), 'U4'], w=['ArbT'])
                A('dve', lambda e, k9v=k9v, h8=h8: e.tensor_tensor(out=AakT[:L, h8, :L], in0=k9v[:L, :, 0, :L], in1=self.Us4[:L, 0:2, :L],
                                                                  op=ALU.mult), r=[('ps', b9), 'Us4'], w=['AakT'])
                bf, kf = self.ps()
                for hh in range(2):
                    self.mm(kf[:L, hh * 128:hh * 128 + L], tf1[:, c, :L], rtmf[:, hh, c, :L], True, True, r=['tf1', 'rtmf'], w=[('ps', bf)])
                kfv = kf[:, 0:256].rearrange("p (h l) -> p h l", h=2)
                A('dve', lambda e, kfv=kfv, h8=h8: e.tensor_tensor(out=ArkT[:L, h8, :L], in0=kfv[:L, :, :L], in1=self.U4[:L, 0:2, :L],
                                                                  op=ALU.mult), r=[('ps', bf), 'U4'], w=['ArkT'])
            for g in range(2):
                b8, k8 = self.ps()
                for hq in range(4):
                    hd = 4 * g + hq
                    c = hd // 2
                    self.mm(k8[:L, hq * 128:hq * 128 + L], art[:, hd % 2, 2 * c, :L], btb[:, c, :L], True, True,
                            r=['art', 'btb'], w=[('ps', b8)])
                k8v = k8[:, :].rearrange("p (h l) -> p h l", h=4)
                A('dve', lambda e, k8v=k8v, g=g: e.tensor_tensor(out=Mth[g][:L, :, :L], in0=k8v[:L, :, :L], in1=self.Ls4[:L, :, :L], op=ALU.mult),
                  r=[('ps', b8), 'Ls4'], w=['Mt%d' % g])
            if EL < 3:
                continue
            self.invert([dict(M=Mh[g], Mn='M%d' % g, Mt=Mth[g], Mtn='Mt%d' % g, Ttb=Ttb[:, 4 * g:4 * g + 4, :], Ttn=('Ttb', g),
                              bufs=((Pb[g], 'Pb%d' % g), (Ptb[g], 'Ptb%d' % g), (TtA[g], 'TtA%d' % g), (TtB[g], 'TtB%d' % g)))
                         for g in range(2)], L)
            if EL < 4:
                continue
            bx, kx = self.ps()
            for hd in range(8):
                c = hd // 2
                self.mm(kx[:L, hd * 64:(hd + 1) * 64], art[:, hd % 2, 2 * c, :L], self.Swb[:, c, :], True, False, r=['art', 'Swb'], w=[('ps', bx)])
                self.mm(kx[:L, hd * 64:(hd + 1) * 64], AakT[:L, hd, :L], vtm[:L, b, hd * 64:(hd + 1) * 64], False, True,
                        r=['AakT', ('vtm', b)], w=[('ps', bx)])
            self.copy('act', Xb[:L].rearrange("p a b -> p (a b)"), kx[:L, :], r=[('ps', bx)], w=['Xb'])
            bu, ku = self.ps()
            for hd in range(8):
                self.mm(ku[:L, hd * 64:(hd + 1) * 64], Ttb[:L, hd, :L], Xb[:L, hd, :], True, True, r=['Ttb', 'Xb'], w=[('ps', bu)])
            self.copy('act', Ub[:L].rearrange("p a b -> p (a b)"), ku[:L, :], r=[('ps', bu)], w=['Ub'])
            by, ky = self.ps()
            for hd in range(8):
                c = hd // 2
                self.mm(ky[:L, hd * 64:(hd + 1) * 64], art[:, hd % 2, 2 * c + 1, :L], self.Swb[:, c, :], True, False, r=['art', 'Swb'], w=[('ps', by)])
                self.mm(ky[:L, hd * 64:(hd + 1) * 64], ArbT[:L, hd, :L], Ub[:L, hd, :], False, False, r=['ArbT', 'Ub'], w=[('ps', by)])
                self.mm(ky[:L, hd * 64:(hd + 1) * 64], ArkT[:L, hd, :L], vtm[:L, b, hd * 64:(hd + 1) * 64], False, True,
                        r=['ArkT', ('vtm', b)], w=[('ps', by)])
            bs, ks = self.ps()
            for c in range(4):
                self.mm(ks[:, c * 128:(c + 1) * 128], bttm[:L, c * 128:(c + 1) * 128], Ub[:L, 2 * c:2 * c + 2, :].rearrange("p a b -> p (a b)"),
                        True, False, r=['bttm', 'Ub'], w=[('ps', bs)])
                self.mm(ks[:, c * 128:(c + 1) * 128], kttm[:L, c * 128:(c + 1) * 128], vtm[:L, b, c * 128:(c + 1) * 128], False, True,
                        r=['kttm', ('vtm', b)], w=[('ps', bs)])
            ksv = ks[:, :].rearrange("p (c d) -> p c d", c=4)
            for hh in range(2):
                hp = hh * 64
                A('dve', lambda e, hp=hp, hh=hh, ksv=ksv: e.tensor_tensor(out=self.Sw[hp:hp + 64, :, :], in0=self.Sw[hp:hp + 64, :, :],
                                                                         in1=ksv[hp:hp + 64, :, hh * 64:(hh + 1) * 64], op=ALU.add),
                  r=['Sw', ('ps', bs), 'Swb'], w=['Sw'])
            for c in range(4):
                A('dve', lambda e, c=c: e.tensor_scalar(out=self.Sw[:, c, :], in0=self.Sw[:, c, :], scalar1=epos[:, c, L - 1:L], scalar2=None,
                                                        op0=ALU.mult), r=['Sw', 'epos'], w=['Sw'])
            self.copy('act', self.Swb[:], self.Sw[:], r=['Sw'], w=['Swb'])
            if EL < 5:
                continue
            kyv = ky[:, :].rearrange("p (h d) -> p h d", h=8)
            A('dve', lambda e, kyv=kyv: e.tensor_reduce(out=st8[:L, :], in_=kyv[:L], axis=AX.X, op=ALU.add), r=[('ps', by)], w=['st8'])
            A('dve', lambda e: e.tensor_scalar(out=st8[:L, :], in0=st8[:L, :], scalar1=-1.0 / 64.0, scalar2=None, op0=ALU.mult), r=['st8'], w=['st8'])
            A('dve', lambda e, kyv=kyv: e.tensor_tensor(out=yA[:L], in0=kyv[:L], in1=st8[:L, :].unsqueeze(2).to_broadcast([L, 8, 64]),
                                                        op=ALU.add), r=[('ps', by), 'st8'], w=['tf1'])
            A('act', lambda e: e.activation(out=yB[:L], in_=yA[:L], func=AF.Square), r=['tf1'], w=['tf2'])
            A('dve', lambda e: e.tensor_reduce(out=st8[:L, :], in_=yB[:L], axis=AX.X, op=ALU.add), r=['tf2'], w=['st8'])
            A('act', lambda e: e.activation(out=st8[:L, :], in_=st8[:L, :], func=AF.Ln, bias=self.epsc[:L, 1:2], scale=1.0 / 64.0),
              r=['st8', 'epsc'], w=['st8'])
            A('act', lambda e: e.activation(out=st8[:L, :], in_=st8[:L, :], func=AF.Exp, scale=-0.5), r=['st8'], w=['st8'])
            A('dve', lambda e: e.tensor_tensor(out=yA[:L], in0=yA[:L], in1=st8[:L, :].unsqueeze(2).to_broadcast([L, 8, 64]), op=ALU.mult),
              r=['tf1', 'st8'], w=['tf1'])
            yAf = yA[:L].rearrange("p a b -> p (a b)")
            yBf = yB[:L].rearrange("p a b -> p (a b)")
            A('dve', lambda e: e.tensor_tensor(out=yAf, in0=yAf, in1=self.lnw_bc[:L, :], op=ALU.mult), r=['tf1', 'lnw_bc'], w=['tf1'])
            A('dve', lambda e: e.tensor_tensor(out=yAf, in0=yAf, in1=self.lnb_bc[:L, :], op=ALU.add), r=['tf1', 'lnb_bc'], w=['tf1'])
            A('dve', lambda e, b=b: e.tensor_tensor(out=yB[:L], in0=vtm[:L, b, :].rearrange("p (h d) -> p h d", h=8),
                                                    in1=srk[:L, :].unsqueeze(2).to_broadcast([L, 8, 64]), op=ALU.mult),
              r=[('vtm', b), 'srk'], w=['tf2'])
            A('dve', lambda e: e.tensor_tensor(out=yAf, in0=yAf, in1=yBf, op=ALU.add), r=['tf1', 'tf2'], w=['tf1'])
            bg, kg = self.ps()
            self.mm(kg[:L, :], sgx[:, bsl], self.lrw[:, 2, :], True, True, r=['sgx', 'lrw'], w=[('ps', bg)])
            A('dve', lambda e, kg=kg: e.tensor_tensor(out=yBf, in0=yAf, in1=kg[:L, :], op=ALU.mult), r=['tf1', ('ps', bg)], w=['tf2'])
            b7, k7 = self.ps()
            for cc in range(4):
                self.tr(k7[:, cc * 128:cc * 128 + L], yBf[:, cc * 128:(cc + 1) * 128], self.ident[:L, :L], r=['tf2', 'ident'], w=[('ps', b7)])
            k7v = k7[:, 0:512].rearrange("p (h l) -> p h l", h=4)
            self.copy('act', ymix[:, 0:4, bsl], k7v[:, :, :L], r=[('ps', b7)], w=['ymix'])
        self.phase(keep)
        if EL < 6:
            return
        qkv = self.wk('qkv', [128, 12, 515], F32)
        zs = self.wk('zs', [128, 4, 512], F32)
        gbr = self.wk('gbr', [128, 4, 8], F32)
        qn = self.wk('qn', [128, 4, 512], F32)
        knf = self.wk('knf', [128, 4, 512], F32)
        knb = self.wk('knb', [128, 4, 512], BF16)
        vt2 = self.wk('vt2', [128, 4, 512], BF16)
        ctmp = [self.wk('ctmp%d' % i, [128, 512], F32) for i in range(3)]
        csq = [self.wk('csq%d' % i, [128, 512], BF16) for i in range(2)]
        crs = [self.wk('crs%d' % i, [128, 512], F32) for i in range(2)]
        self.copy('dve', qkv[:, :, 0:3], self.convc[:], r=['convc'], w=['qkv'])
        for t in range(3):
            wv, wtok = self.wload(wi[4 + t])
            for j in range(4):
                c = 4 * t + j
                self.proj_fm(wv, wtok, j * 128, 128, N,
                             lambda bank, bi, c=c: self.copy(self.ev_eng(), qkv[:, c, 3:N + 3], bank[:, :N], r=[('ps', bi)], w=[('qkv', c)]))
        wv, wtok = self.wload(wi[7])
        for b in range(nb):
            self.proj_tm(wv, wtok, 0, 8, b, L,
                         lambda bank, bi, b=b: self.copy('dve', gbr[:L, b, :], bank[:L, 0:8], r=[('ps', bi)], w=[('gbr', b)]))
        wv, wtok = self.wload(wi[8])
        for b in range(nb):
            def evz(bank, bi, b=b):
                A('act', lambda e: e.activation(out=zs[:L, b, :], in_=bank[:L, :], func=AF.Silu), r=[('ps', bi)], w=[('zs', b)])
                zsv = zs[:L, b, :].rearrange("p (h d) -> p h d", h=4)
                A('dve', lambda e: e.tensor_tensor(out=zsv, in0=zsv, in1=self.dnw4[:L], op=ALU.mult), r=[('zs', b), 'dnw4'], w=[('zs', b)])
            self.proj_tm(wv, wtok, 0, 512, b, L, evz)
        for c in range(12):
            ct = ctmp[c % 3]
            ctn = 'ctmp%d' % (c % 3)
            A('act', lambda e, c=c, ct=ct: e.activation(out=ct[:, :N], in_=qkv[:, c, 0:N], func=AF.Copy, scale=self.cw_col[:, c, 0:1]),
              r=[('qkv', c), 'qkv', 'cw_col'], w=[ctn])
            for j in range(1, 4):
                A('dve', lambda e, c=c, ct=ct, j=j: e.scalar_tensor_tensor(out=ct[:, :N], in0=qkv[:, c, j:N + j], scalar=self.cw_col[:, c, j:j + 1],
                                                                           in1=ct[:, :N], op0=ALU.mult, op1=ALU.add),
                  r=[('qkv', c), 'qkv', 'cw_col', ctn], w=[ctn])
            self.copy('act', self.convc[:, c, :], qkv[:, c, N:N + 3], r=[('qkv', c)], w=['convc'])
            A('act', lambda e, c=c, ct=ct: e.activation(out=qkv[:, c, 3:N + 3], in_=ct[:, :N], func=AF.Silu), r=[ctn, ('qkv', c)], w=[('qkv', c)])
            if c >= 8:
                for b in range(nb):
                    bi, bank = self.ps()
                    self.tr(bank[:L, 0:128], qkv[:, c, 3 + b * L:3 + (b + 1) * L], self.ident[:], r=[('qkv', c), 'ident'], w=[('ps', bi)])
                    self.copy('dve' if b % 2 else 'act', vt2[:L, b, (c - 8) * 128:(c - 7) * 128], bank[:L, 0:128], r=[('ps', bi)], w=[('vt2', b)])
        for c in range(8):
            cs = csq[c % 2]
            csn = 'csq%d' % (c % 2)
            cr = crs[c % 2]
            crn = 'crs%d' % (c % 2)
            A('act', lambda e, c=c, cs=cs: e.activation(out=cs[:, :N], in_=qkv[:, c, 3:N + 3], func=AF.Square), r=[('qkv', c)], w=[csn])
            bq, kq = self.ps()
            self.mm(kq[:, :N], self.ones_bf[:], cs[:, :N], True, True, r=[csn, 'ones_bf'], w=[('ps', bq)])
            A('act', lambda e, kq=kq, cr=cr: e.activation(out=cr[:, :N], in_=kq[:, :N], func=AF.Ln, bias=self.epsc[:, 0:1]),
              r=[('ps', bq), 'epsc'], w=[crn])
            A('act', lambda e, cr=cr: e.activation(out=cr[:, :N], in_=cr[:, :N], func=AF.Exp, scale=-0.5), r=[crn], w=[crn])
            if c < 4:
                A('dve', lambda e, c=c, cr=cr: e.scalar_tensor_tensor(out=qn[:, c, :N], in0=qkv[:, c, 3:N + 3], scalar=float(128 ** -0.5),
                                                                      in1=cr[:, :N], op0=ALU.mult, op1=ALU.mult), r=[('qkv', c), crn], w=[('qn', c)])
            else:
                A('dve', lambda e, c=c, cr=cr: e.tensor_tensor(out=knf[:, c - 4, :N], in0=qkv[:, c, 3:N + 3], in1=cr[:, :N], op=ALU.mult),
                  r=[('qkv', c), crn], w=[('knf', c - 4)])
                self.copy('act', knb[:, c - 4, :N], knf[:, c - 4, :N], r=[('knf', c - 4)], w=[('knb', c - 4)])
        gcol = self.wk('gcol', [128, 16], F32)
        gtm = self.wk('gtm', [128, 8], F32)
        dgds = [self.wk('dgd%d' % i, [128, 128], F32) for i in range(8)]
        d1s = self.wk('d1s', [128, 4, 128], F32)
        eG = self.wk('eG', [128, 4, 128], F32)
        Ei = self.wk('Ei', [128, 4, 128], F32)
        Es = self.wk('Es', [128, 4, 128], F32)
        dM = self.wk('dM', [128, 4, 128], F32)
        dMt = self.wk('dMt', [128, 4, 128], F32)
        QKm = self.wk('QKm', [128, 4, 128], BF16)
        dPb = self.wk('Pb0', [128, 4, 128], F32)
        dPtb = self.wk('Ptb0', [128, 4, 128], F32)
        dTtA = self.wk('TtA0', [128, 4, 128], F32)
        dTtB = self.wk('TtB0', [128, 4, 128], F32)
        dTtb = self.wk('dTtb', [128, 4, 128], BF16)
        qg = self.wk('qg', [128, 4, 128], BF16)
        kg_ = self.wk('kg', [128, 4, 128], BF16)
        kdtm = self.wk('kdtm', [128, 4, 128], BF16)
        dX = self.wk('dX', [128, 4, 128], BF16)
        dU = self.wk('dU', [128, 4, 128], BF16)
        gl = self.wk('gl', [128, 8], F32)
        osq = self.wk('osq', [128, 4, 128], F32)
        ss = self.wk('ss', [128, 4], F32)
        ytm = self.wk('ytm', [128, 4, 128], F32)
        print('delta work end', self.woff)
        for b in range(nb if EL >= 7 else 0):
            bsl = slice(b * L, (b + 1) * L)
            A('dve', lambda e, b=b: e.tensor_tensor(out=gtm[:L, 0:4], in0=gbr[:L, b, 0:4], in1=self.dtb_bc[:L, :], op=ALU.add),
              r=[('gbr', b), 'dtb_bc'], w=['gtm'])
            A('act', lambda e: e.activation(out=gtm[:L, 0:4], in_=gtm[:L, 0:4], func=AF.Exp), r=['gtm'], w=['gtm'])
            A('act', lambda e: e.activation(out=gtm[:L, 0:4], in_=gtm[:L, 0:4], func=AF.Ln, bias=self.epsc[:L, 2:3]), r=['gtm', 'epsc'], w=['gtm'])
            A('dve', lambda e: e.tensor_tensor(out=gtm[:L, 0:4], in0=gtm[:L, 0:4], in1=self.nexpA[:L, :], op=ALU.mult), r=['gtm', 'nexpA'], w=['gtm'])
            A('act', lambda e, b=b: e.activation(out=gcol[:L, 4:8], in_=gbr[:L, b, 4:8], func=AF.Exp, scale=-1.0), r=[('gbr', b)], w=['gcol'])
            A('dve', lambda e: e.tensor_scalar(out=gcol[:L, 4:8], in0=gcol[:L, 4:8], scalar1=1.0, scalar2=None, op0=ALU.add), r=['gcol'], w=['gcol'])
            A('dve', lambda e: e.reciprocal(out=gcol[:L, 4:8], in_=gcol[:L, 4:8]), r=['gcol'], w=['gcol'])
            A('dve', lambda e: e.tensor_scalar(out=gcol[:L, 8:12], in0=gcol[:L, 4:8], scalar1=-1.0, scalar2=None, op0=ALU.mult), r=['gcol'], w=['gcol'])
            b1, k1 = self.ps()
            A('pe', lambda e, k1=k1: e.matmul(k1[:L, 0:4], lhsT=self.U1[:L, :L], rhs=gtm[:L, 0:4], start=True, stop=True),
              r=['gtm', 'U4'], w=[('ps', b1)])
            self.copy('dve', gcol[:L, 0:4], k1[:L, 0:4], r=[('ps', b1)], w=['gcol'])
            b3, k3 = self.ps()
            b4, k4 = self.ps()
            for hd in range(4):
                for (col, kk_, bb_) in ((hd, k3, b3), (4 + hd, k4, b4)):
                    dgd = dgds[col]
                    A('dve', lambda e, col=col, dgd=dgd: e.tensor_scalar(out=dgd[:L, :L], in0=self.ident[:L, :L], scalar1=gcol[:L, col:col + 1],
                                                                         scalar2=None, op0=ALU.mult), r=['gcol', 'ident'], w=['dgd%d' % col])
            for hd in range(4):
                for (col, kk_, bb_) in ((hd, k3, b3), (4 + hd, k4, b4)):
                    dgd = dgds[col]
                    A('pe', lambda e, hd=hd, kk_=kk_, dgd=dgd: e.matmul(kk_[:, hd * 128:hd * 128 + L], lhsT=self.ones_f[:L, :], rhs=dgd[:L, :L],
                                                                        start=True, stop=True), r=['dgd%d' % col, 'ones_f'], w=[('ps', bb_)])
            k3v = k3[:, :].rearrange("p (h l) -> p h l", h=4)
            k4v = k4[:, :].rearrange("p (h l) -> p h l", h=4)
            for hd in range(4):
                A('dve', lambda e, hd=hd, k3v=k3v: e.scalar_tensor_tensor(out=d1s[:L, hd, :L], in0=k3v[:L, hd, :L], scalar=gcol[:L, hd:hd + 1],
                                                                          in1=self.U4[:L, hd, :L], op0=ALU.subtract, op1=ALU.mult),
                  r=[('ps', b3), 'gcol', 'U4'], w=['d1s'])
            A('act', lambda e: e.activation(out=d1s[:L, :, :L], in_=d1s[:L, :, :L], func=AF.Exp), r=['d1s'], w=['d1s'])
            A('act', lambda e, k3v=k3v: e.activation(out=eG[:, :, :L], in_=k3v[:, :, :L], func=AF.Exp), r=[('ps', b3)], w=['eG'])
            A('dve', lambda e, k3v=k3v: e.tensor_tensor(out=gcol[:L, 12:16], in0=k3v[:L, :, L - 1], in1=gcol[:L, 0:4], op=ALU.subtract),
              r=[('ps', b3), 'gcol'], w=['gcol'])
            A('act', lambda e: e.activation(out=gcol[:L, 12:16], in_=gcol[:L, 12:16], func=AF.Exp), r=['gcol'], w=['gcol'])
            A('act', lambda e, k3v=k3v: e.activation(out=gl[:, 0:4], in_=k3v[:, :, L - 1], func=AF.Exp), r=[('ps', b3)], w=['gl'])
            A('dve', lambda e: e.tensor_tensor(out=Ei[:L, :, :L], in0=d1s[:L, :, :L], in1=self.U4[:L, :, :L], op=ALU.mult), r=['d1s', 'U4'], w=['Ei'])
            A('dve', lambda e: e.tensor_tensor(out=Es[:L, :, :L], in0=d1s[:L, :, :L], in1=self.Us4[:L, :, :L], op=ALU.mult), r=['d1s', 'Us4'], w=['Es'])
            b5, k5 = self.ps()
            b6, k6 = self.ps()
            for hd in range(4):
                self.mm(k5[:L, hd * 128:hd * 128 + L], knb[:, hd, bsl], knb[:, hd, bsl], True, True, r=[('knb', hd)], w=[('ps', b5)])
                self.mm(k6[:L, hd * 128:hd * 128 + L], knf[:, hd, bsl], qn[:, hd, bsl], True, True, r=[('knf', hd), ('qn', hd)], w=[('ps', b6)])
            k5v = k5[:, :].rearrange("p (h l) -> p h l", h=4)
            k6v = k6[:, :].rearrange("p (h l) -> p h l", h=4)
            A('dve', lambda e, k5v=k5v: e.tensor_tensor(out=Es[:L, :, :L], in0=k5v[:L, :, :L], in1=Es[:L, :, :L], op=ALU.mult), r=[('ps', b5), 'Es'], w=['Es'])
            A('dve', lambda e, k4v=k4v: e.scalar_tensor_tensor(out=dM[:L, :, :L], in0=Es[:L, :, :L], scalar=-1.0, in1=k4v[:L, :, :L],
                                                               op0=ALU.mult, op1=ALU.mult), r=['Es', ('ps', b4)], w=['dM'])
            A('dve', lambda e, k6v=k6v: e.tensor_tensor(out=QKm[:L, :, :L], in0=k6v[:L, :, :L], in1=Ei[:L, :, :L], op=ALU.mult),
              r=[('ps', b6), 'Ei'], w=['QKm'])
            b7, k7 = self.ps()
            for hd in range(4):
                self.tr(k7[:L, hd * 128:hd * 128 + L], dM[:L, hd, :L], self.ident[:L, :L], r=['dM', 'ident'], w=[('ps', b7)])
            k7v = k7[:, :].rearrange("p (h l) -> p h l", h=4)
            self.copy('act', dMt[:L, :, :L], k7v[:L, :, :L], r=[('ps', b7)], w=['dMt'])
            if EL < 8:
                continue
            self.invert([dict(M=dM, Mn='dM', Mt=dMt, Mtn='dMt', Ttb=dTtb, Ttn='dTtb',
                              bufs=((dPb, 'Pb0'), (dPtb, 'Ptb0'), (dTtA, 'TtA0'), (dTtB, 'TtB0')))], L)
            if EL < 9:
                continue
            A('dve', lambda e, bsl=bsl: e.tensor_tensor(out=qg[:, :, :L], in0=qn[:, :, bsl], in1=eG[:, :, :L], op=ALU.mult), r=['qn', 'eG'], w=['qg'])
            A('dve', lambda e, bsl=bsl: e.tensor_tensor(out=kg_[:, :, :L], in0=knb[:, :, bsl], in1=eG[:, :, :L], op=ALU.mult), r=['knb', 'eG'], w=['kg'])
            b8, k8 = self.ps()
            for hd in range(4):
                self.tr(k8[:L, hd * 128:(hd + 1) * 128], knf[:, hd, bsl], self.ident[:], r=[('knf', hd), 'ident'], w=[('ps', b8)])
            for hd in range(4):
                A('dve', lambda e, hd=hd, k8=k8: e.tensor_scalar(out=kdtm[:L, hd, :], in0=k8[:L, hd * 128:(hd + 1) * 128],
                                                                 scalar1=gcol[:L, 12 + hd:13 + hd], scalar2=None, op0=ALU.mult),
                  r=[('ps', b8), 'gcol'], w=['kdtm'])
            b9, k9 = self.ps()
            for hd in range(4):
                self.mm(k9[:L, hd * 128:(hd + 1) * 128], kg_[:, hd, :L], self.Sdb[:, hd, :], True, True, r=['kg', 'Sdb'], w=[('ps', b9)])
            for hd in range(4):
                A('dve', lambda e, hd=hd, k9=k9, b=b: e.tensor_tensor(out=osq[:L, hd, :], in0=k9[:L, hd * 128:(hd + 1) * 128],
                                                                     in1=vt2[:L, b, hd * 128:(hd + 1) * 128], op=ALU.subtract),
                  r=[('ps', b9), ('vt2', b)], w=['osq'])
                A('dve', lambda e, hd=hd: e.tensor_scalar(out=dX[:L, hd, :], in0=osq[:L, hd, :], scalar1=gcol[:L, 8 + hd:9 + hd], scalar2=None,
                                                          op0=ALU.mult), r=['osq', 'gcol'], w=['dX'])
            bu, ku = self.ps()
            for hd in range(4):
                self.mm(ku[:L, hd * 128:(hd + 1) * 128], dTtb[:L, hd, :L], dX[:L, hd, :], True, True, r=['dTtb', 'dX'], w=[('ps', bu)])
            self.copy('act', dU[:L].rearrange("p a b -> p (a b)"), ku[:L, :], r=[('ps', bu)], w=['dU'])
            bo, ko = self.ps()
            for hd in range(4):
                self.mm(ko[:L, hd * 128:(hd + 1) * 128], qg[:, hd, :L], self.Sdb[:, hd, :], True, False, r=['qg', 'Sdb'], w=[('ps', bo)])
                self.mm(ko[:L, hd * 128:(hd + 1) * 128], QKm[:L, hd, :L], dU[:L, hd, :], False, True, r=['QKm', 'dU'], w=[('ps', bo)])
            bs, ks = self.ps()
            for hd in range(4):
                self.mm(ks[:, hd * 128:(hd + 1) * 128], kdtm[:L, hd, :], dU[:L, hd, :], True, True, r=['kdtm', 'dU'], w=[('ps', bs)])
            for hd in range(4):
                A('dve', lambda e, hd=hd, ks=ks: e.scalar_tensor_tensor(out=self.Sd[:, hd, :], in0=self.Sd[:, hd, :], scalar=gl[:, hd:hd + 1],
                                                                        in1=ks[:, hd * 128:(hd + 1) * 128], op0=ALU.mult, op1=ALU.add),
                  r=['Sd', 'gl', ('ps', bs), 'Sdb'], w=['Sd'])
            self.copy('act', self.Sdb[:], self.Sd[:], r=['Sd'], w=['Sdb'])
            self.head_rms_post(ko, bo, L, zs[:L, b, :].rearrange("p (h d) -> p h d", h=4), ('zs', b), ytm, 'ytm', osq, ss,
                               ymix[:, 4:8, bsl], 'ymix', bsl, 0, 4, 128)
        self.linear_resid(self.even_wo, ymix, 'ymix', N)

def make_in_map(b, inp, core, seq):
    ns = b.n_samp
    m = {}
    m['xp'] = np.ascontiguousarray(inp['x_prompt'][core, :seq])
    m['memp'] = np.ascontiguousarray(inp['mem_prompt'][core])
    sl = slice(core * ns, (core + 1) * ns)
    m['xs'] = np.ascontiguousarray(inp['x_sample'][sl])
    m['ck'] = np.ascontiguousarray(inp['cache_mem_k'][:, sl].reshape(2, ns, NMEM, D))
    m['cv'] = np.ascontiguousarray(inp['cache_mem_v'][:, sl].reshape(2, ns, NMEM, D))
    for nm in b.w_in:
        m[nm] = np.ascontiguousarray(inp[nm])
    m['s0_shift'] = np.ascontiguousarray(inp['state_rwkv_shift'][0, sl, 0])
    m['s0_rwkv'] = np.ascontiguousarray(inp['state_rwkv'][0, sl])
    m['s0_conv'] = np.ascontiguousarray(inp['state_delta_conv'][0, sl])
    m['s0_delta'] = np.ascontiguousarray(inp['state_delta'][0, sl])
    m['s0_gla'] = np.ascontiguousarray(inp['state_gla'][0, sl])
    m['s0_ret'] = np.ascontiguousarray(inp['state_ret'][0, sl])
    return m


SEQ_FULL = 8192
N_CORES = 8
USE = ('ffn', 'attn', 'odd', 'even')
_OUT_ORDER = ['yp', 'ys', 'p_shift', 'p_rwkv', 'p_conv', 'p_delta', 'p_gla', 'p_ret', 'p_mk', 'p_mv',
              's_shift', 's_rwkv', 's_conv', 's_delta', 's_gla', 's_ret']


def kernel(**inputs):
    inp = {k: np.asarray(v) for k, v in inputs.items()}
    b = Builder(SEQ_FULL, n_samp=2, use=USE)
    in_maps = [make_in_map(b, inp, c, SEQ_FULL) for c in range(N_CORES)]
    in_maps = [{k: m[k] for k in b.in_names} for m in in_maps]
    res = run_bass_kernel_spmd(b.nc, in_maps, core_ids=list(range(N_CORES)))
    R = res.results
    f = np.float32

    def cat(name, shape_core):
        return np.stack([np.asarray(r[name], f).reshape(shape_core) for r in R])

    def get(name, shape):
        if name in R[0]:
            return cat(name, shape)
        return np.zeros((N_CORES,) + tuple(shape), f)
    yp = get('yp', (SEQ_FULL, D))
    ys = get('ys', (2, 16, D)).reshape(16, 16, D)
    p_shift = get('p_shift', (1, A_PROJ))[None]
    p_rwkv = get('p_rwkv', (8, 64, 64))[None]
    p_conv = get('p_conv', (3, B_CONV))[None]
    p_delta = get('p_delta', (4, 128, 128))[None]
    p_gla = get('p_gla', (4, 64, 128))[None]
    p_ret = get('p_ret', (4, 128, 128))[None]
    p_mk = np.transpose(get('p_mk', (2, NMEM, 4, 256)), (1, 0, 2, 3, 4))
    p_mv = np.transpose(get('p_mv', (2, NMEM, 4, 256)), (1, 0, 2, 3, 4))
    s_shift = get('s_shift', (2, 1, A_PROJ)).reshape(1, 16, 1, A_PROJ)
    s_rwkv = get('s_rwkv', (2, 8, 64, 64)).reshape(1, 16, 8, 64, 64)
    s_conv = get('s_conv', (2, 3, B_CONV)).reshape(1, 16, 3, B_CONV)
    s_delta = get('s_delta', (2, 4, 128, 128)).reshape(1, 16, 4, 128, 128)
    s_gla = get('s_gla', (2, 4, 64, 128)).reshape(1, 16, 4, 64, 128)
    s_ret = get('s_ret', (2, 4, 128, 128)).reshape(1, 16, 4, 128, 128)
    return (yp, ys, p_shift, p_rwkv, p_conv, p_delta, p_gla, p_ret, np.ascontiguousarray(p_mk), np.ascontiguousarray(p_mv),
            s_shift, s_rwkv, s_conv, s_delta, s_gla, s_ret)
```
